# Optimizing a Trainium2 kernel written in Bass

```python
import jax, jax.numpy as jnp
from jax import lax
import numpy as np

D_MODEL = 2048
BATCH = 4
SEQ = 4096
DEPTH = 4

CHUNK = 64
PLE_DIM = 256
A_HEADS = 8
A_HEAD_DIM = 128
A_WIDTH = A_HEADS * A_HEAD_DIM
B_HEADS = 8
B_HEAD_DIM = 128
B_WIDTH = B_HEADS * B_HEAD_DIM
IDX_HEADS = 16
IDX_DIM = 64
TOPK_MAX = 256
QBLOCK = 128
ROPE_THETA = 10000.0
D_FF = ((8 * D_MODEL // 3 + 255) // 256) * 256
ALPHA = (2 * DEPTH) ** 0.25
BETA = (8 * DEPTH) ** -0.25
LN_EPS = 1e-5
RMS_EPS = 1e-6
MASK_VALUE = -1e30
ATTN_SCALE = B_HEAD_DIM ** -0.5
IDX_SCALE = (IDX_HEADS * IDX_DIM) ** -0.5
IN_SPLITS = (A_WIDTH, A_WIDTH, A_WIDTH, A_WIDTH,
             B_WIDTH, B_WIDTH, B_WIDTH,
             IDX_HEADS * IDX_DIM, IDX_DIM, IDX_HEADS,
             D_MODEL, D_MODEL)
IN_WIDTH = sum(IN_SPLITS)

kernel_name = "hgrn2_dsa_gated_hybrid_deepnorm"


def layer_norm(x, g, b):
    xf = x.astype(jnp.float32)
    mu = jnp.mean(xf, axis=-1, keepdims=True)
    xc = xf - mu
    var = jnp.mean(xc * xc, axis=-1, keepdims=True)
    return (xc * lax.rsqrt(var + LN_EPS) * g.astype(jnp.float32) + b.astype(jnp.float32)).astype(x.dtype)


def rope_tables(positions, dim):
    half = dim // 2
    inv_freq = ROPE_THETA ** (-jnp.arange(half, dtype=jnp.float32) / half)
    ang = positions.astype(jnp.float32)[..., None] * inv_freq
    return jnp.cos(ang), jnp.sin(ang)


def apply_rope(x, cos, sin):
    half = x.shape[-1] // 2
    xf = x.astype(jnp.float32)
    x1, x2 = xf[..., :half], xf[..., half:]
    return jnp.concatenate([x1 * cos - x2 * sin, x1 * sin + x2 * cos], axis=-1).astype(x.dtype)


def hgrn2_mixer(q, f_pre, i_in, g, lb, norm_g):
    f32 = jnp.float32
    bsz, seq, _ = q.shape
    n_chunks = seq // CHUNK
    f_pre = f_pre.astype(f32)
    lb = lb.astype(f32)
    log_f = jax.nn.log_sigmoid(f_pre) + jnp.log1p(lb * jnp.exp(-f_pre))
    k = (1.0 - lb) * jax.nn.sigmoid(-f_pre)
    qf = jax.nn.silu(q.astype(f32))

    def to_chunks(t):
        return t.reshape(bsz, n_chunks, CHUNK, A_HEADS, A_HEAD_DIM).transpose(1, 0, 3, 2, 4)

    xs = (to_chunks(qf), to_chunks(k), to_chunks(i_in.astype(f32)), to_chunks(log_f))
    causal = jnp.tril(jnp.ones((CHUNK, CHUNK), dtype=bool))

    def step(state, inp):
        qc, kc, vc, gc = inp
        cum = jnp.cumsum(gc, axis=2)
        o_inter = jnp.einsum('bhtk,bhkv->bhtv', qc * jnp.exp(cum), state)
        diff = cum[:, :, :, None, :] - cum[:, :, None, :, :]
        decay = jnp.exp(jnp.where(causal[:, :, None], diff, MASK_VALUE))
        scores = jnp.einsum('bhtk,bhsk,bhtsk->bhts', qc, kc, decay)
        o = o_inter + jnp.einsum('bhts,bhsv->bhtv', scores, vc)
        last = cum[:, :, -1:, :]
        new_state = jnp.exp(last[:, :, 0, :])[..., None] * state + \
            jnp.einsum('bhsk,bhsv->bhkv', kc * jnp.exp(last - cum), vc)
        return new_state, o

    state0 = jnp.zeros((bsz, A_HEADS, A_HEAD_DIM, A_HEAD_DIM), f32)
    _, o = lax.scan(step, state0, xs)
    o = o.transpose(1, 0, 3, 2, 4).reshape(bsz, seq, A_HEADS, A_HEAD_DIM)
    o = o * lax.rsqrt(jnp.mean(o * o, axis=-1, keepdims=True) + RMS_EPS) * norm_g.astype(f32)
    gate = jax.nn.silu(g.astype(f32)).reshape(bsz, seq, A_HEADS, A_HEAD_DIM)
    return (o * gate).reshape(bsz, seq, A_WIDTH).astype(q.dtype)


def dsa_mixer(q, k, v, q_idx, k_idx, w_idx, cos_h, sin_h, cos_i, sin_i, kn_g, kn_b):
    f32 = jnp.float32
    bsz, seq, _ = q.shape
    topk = min(TOPK_MAX, seq // 4)
    n_blocks = seq // QBLOCK
    q = apply_rope(q.reshape(bsz, seq, B_HEADS, B_HEAD_DIM), cos_h[:, :, None], sin_h[:, :, None])
    k = apply_rope(k.reshape(bsz, seq, B_HEADS, B_HEAD_DIM), cos_h[:, :, None], sin_h[:, :, None])
    v = v.reshape(bsz, seq, B_HEADS, B_HEAD_DIM)
    q_idx = apply_rope(q_idx.reshape(bsz, seq, IDX_HEADS, IDX_DIM),
                       cos_i[:, :, None], sin_i[:, :, None]).astype(f32)
    k_idx = apply_rope(layer_norm(k_idx, kn_g, kn_b), cos_i, sin_i).astype(f32)
    w_idx = w_idx.astype(f32) * IDX_SCALE
    key_chunk = jnp.arange(seq) // CHUNK
    gather = jax.vmap(lambda t, idx: t[idx])

    def to_blocks(t):
        return t.reshape(bsz, n_blocks, QBLOCK, *t.shape[2:]).swapaxes(0, 1)

    def one_block(blk):
        qb, qib, wib, start = blk
        q_chunk = (start + jnp.arange(QBLOCK)) // CHUNK
        rel = jax.nn.relu(jnp.einsum('bqhd,bsd->bqhs', qib, k_idx))
        score = jnp.einsum('bqh,bqhs->bqs', wib, rel)
        score = jnp.where(key_chunk[None, None, :] <= q_chunk[None, :, None], score, MASK_VALUE)
        _, sel = lax.top_k(score, topk)
        ks = gather(k, sel)
        vs = gather(v, sel)
        valid = key_chunk[sel] <= q_chunk[None, :, None]
        logits = jnp.einsum('bqhd,bqkhd->bqhk', qb, ks).astype(f32) * ATTN_SCALE
        logits = jnp.where(valid[:, :, None, :], logits, MASK_VALUE)
        probs = jax.nn.softmax(logits, axis=-1).astype(vs.dtype)
        return jnp.einsum('bqhk,bqkhd->bqhd', probs, vs)

    starts = jnp.arange(n_blocks, dtype=jnp.int32) * QBLOCK
    out = lax.map(one_block, (to_blocks(q), to_blocks(q_idx), to_blocks(w_idx), starts))
    return out.swapaxes(0, 1).reshape(bsz, seq, B_WIDTH)


def setup_inputs(seed: int = 0) -> dict:
    key = jax.random.key(seed)
    ks = jax.random.split(key, 24)
    f32 = jnp.float32

    def normal(k, shape, scale):
        return jax.random.normal(k, shape, f32) * scale

    def gain(k, shape):
        return 1.0 + 0.01 * jax.random.normal(k, shape, f32)

    def bias(k, shape):
        return 0.01 * jax.random.normal(k, shape, f32)

    x = normal(ks[0], (BATCH, SEQ, D_MODEL), 1.0)
    p = normal(ks[1], (DEPTH, BATCH, SEQ, PLE_DIM), 1.0)
    offsets = jax.random.randint(ks[2], (BATCH, 1), 0, 64, dtype=jnp.int32) * CHUNK
    positions = (offsets + jnp.arange(SEQ, dtype=jnp.int32)[None, :]).astype(jnp.int32)
    return {
        "x": x,
        "p": p,
        "positions": positions,
        "w_in": normal(ks[3], (DEPTH, D_MODEL, IN_WIDTH), D_MODEL ** -0.5),
        "w_branch_a": normal(ks[4], (DEPTH, A_WIDTH, D_MODEL), A_WIDTH ** -0.5),
        "w_branch_b": normal(ks[5], (DEPTH, B_WIDTH, D_MODEL), B_WIDTH ** -0.5),
        "w_out": normal(ks[6], (DEPTH, D_MODEL, D_MODEL), BETA * D_MODEL ** -0.5),
        "hgrn_lower_bounds": normal(ks[7], (DEPTH, A_WIDTH), 0.1),
        "hgrn_norm_g": gain(ks[8], (DEPTH, A_HEAD_DIM)),
        "idx_k_norm_g": gain(ks[9], (DEPTH, IDX_DIM)),
        "idx_k_norm_b": bias(ks[10], (DEPTH, IDX_DIM)),
        "ln_mix_g": gain(ks[11], (DEPTH, D_MODEL)),
        "ln_mix_b": bias(ks[12], (DEPTH, D_MODEL)),
        "w_ffn_gate": normal(ks[13], (DEPTH, D_MODEL, D_FF), D_MODEL ** -0.5),
        "w_ffn_up": normal(ks[14], (DEPTH, D_MODEL, D_FF), D_MODEL ** -0.5),
        "w_ffn_down": normal(ks[15], (DEPTH, D_FF, D_MODEL), BETA * D_FF ** -0.5),
        "ln_ffn_g": gain(ks[16], (DEPTH, D_MODEL)),
        "ln_ffn_b": bias(ks[17], (DEPTH, D_MODEL)),
        "w_ple_gate": normal(ks[18], (DEPTH, D_MODEL, D_MODEL), D_MODEL ** -0.5),
        "w_ple_proj": normal(ks[19], (DEPTH, PLE_DIM, D_MODEL), BETA * PLE_DIM ** -0.5),
        "ln_ple_g": gain(ks[20], (DEPTH, D_MODEL)),
        "ln_ple_b": bias(ks[21], (DEPTH, D_MODEL)),
    }


def reference(x, p, positions, w_in, w_branch_a, w_branch_b, w_out, hgrn_lower_bounds,
              hgrn_norm_g, idx_k_norm_g, idx_k_norm_b, ln_mix_g, ln_mix_b,
              w_ffn_gate, w_ffn_up, w_ffn_down, ln_ffn_g, ln_ffn_b,
              w_ple_gate, w_ple_proj, ln_ple_g, ln_ple_b):
    cos_h, sin_h = rope_tables(positions, B_HEAD_DIM)
    cos_i, sin_i = rope_tables(positions, IDX_DIM)
    lb_sm = jax.nn.softmax(hgrn_lower_bounds.astype(jnp.float32), axis=0)
    lower_bounds = jnp.cumsum(lb_sm, axis=0) - lb_sm[0]
    split_at = np.cumsum(IN_SPLITS)[:-1].tolist()
    for layer in range(DEPTH):
        (a_q, a_f, a_i, a_g, b_q, b_k, b_v, i_q, i_k, i_w, gate_a, gate_b) = \
            jnp.split(x @ w_in[layer], split_at, axis=-1)
        y_a = hgrn2_mixer(a_q, a_f, a_i, a_g, lower_bounds[layer], hgrn_norm_g[layer])
        y_b = dsa_mixer(b_q, b_k, b_v, i_q, i_k, i_w, cos_h, sin_h, cos_i, sin_i,
                        idx_k_norm_g[layer], idx_k_norm_b[layer])
        merged = jax.nn.sigmoid(gate_a) * (y_a @ w_branch_a[layer]) + \
            jax.nn.sigmoid(gate_b) * (y_b @ w_branch_b[layer])
        x = layer_norm(ALPHA * x + merged @ w_out[layer], ln_mix_g[layer], ln_mix_b[layer])
        ffn = (jax.nn.silu(x @ w_ffn_gate[layer]) * (x @ w_ffn_up[layer])) @ w_ffn_down[layer]
        x = layer_norm(ALPHA * x + ffn, ln_ffn_g[layer], ln_ffn_b[layer])
        ple = jax.nn.sigmoid(x @ w_ple_gate[layer]) * (p[layer] @ w_ple_proj[layer])
        x = layer_norm(ALPHA * x + ple, ln_ple_g[layer], ln_ple_b[layer])
    return x
```

```python
import numpy as np
import concourse.bass as bass
import concourse.mybir as mybir
from concourse.bass_utils import run_bass_kernel_spmd

F32 = mybir.dt.float32
BF16 = mybir.dt.bfloat16
I32 = mybir.dt.int32
AF = mybir.ActivationFunctionType
ALU = mybir.AluOpType
AX = mybir.AxisListType

D_MODEL = 2048
SEQ = 4096
DEPTH = 4
CHUNK = 64
PLE_DIM = 256
A_W = 1024
B_W = 1024
IDX_HEADS = 16
IDX_DIM = 64
TOPK = 256
D_FF = 5632
ALPHA = (2 * DEPTH) ** 0.25
LN_EPS = 1e-5
RMS_EPS = 1e-6
ATTN_SCALE = 128 ** -0.5
IDX_SCALE = (IDX_HEADS * IDX_DIM) ** -0.5
IN_W = 12368
KC = D_MODEL // 128
NBLK = SEQ // 128
TT = 512
NEG = -1.0e30


class Buf:
    __slots__ = ("name", "writer", "readers")

    def __init__(self, name):
        self.name = name
        self.writer = None
        self.readers = []


class Trk:
    def __init__(self, nc, n_dma_sems=40):
        self.nc = nc
        self.eng = {"pe": nc.tensor, "act": nc.scalar, "dve": nc.vector, "pool": nc.gpsimd, "sp": nc.sync}
        self.sems = {}
        self.count = {}
        for e in ("pe", "act", "dve", "pool"):
            self.sems[e] = nc.alloc_semaphore("s_" + e)
            self.count[e] = 0
        self.dma_keys = []
        for i in range(n_dma_sems):
            k = "d%d" % i
            self.sems[k] = nc.alloc_semaphore("s_" + k)
            self.count[k] = 0
            self.dma_keys.append(k)
        self.dma_rr = 0
        self.qkeys = {"sp": self.dma_keys[:28], "pool": self.dma_keys[28:]}
        self.qrr = {"sp": 0, "pool": 0}
        self.waited = {e: {} for e in self.eng}
        self.nbuf = 0
        self.ninst = 0

    def buf(self, name=None):
        self.nbuf += 1
        return Buf(name or "b%d" % self.nbuf)

    def bufs(self, n, name="b"):
        return [self.buf("%s%d" % (name, i)) for i in range(n)]

    def _wait(self, e, tok):
        if tok is None:
            return
        k, v = tok
        if e == "pe" and k == "pe":
            return
        w = self.waited[e]
        if w.get(k, 0) >= v:
            return
        w[k] = v
        self.eng[e].wait_ge(self.sems[k], v)

    def _deps(self, e, reads, writes):
        for b in reads:
            self._wait(e, b.writer)
        for b in writes:
            self._wait(e, b.writer)
            for r in b.readers:
                self._wait(e, r)

    def _commit(self, tok, reads, writes):
        for b in reads:
            b.readers.append(tok)
        for b in writes:
            b.writer = tok
            b.readers = []

    def op(self, e, fn, reads=(), writes=()):
        self._deps(e, reads, writes)
        ins = fn()
        self.count[e] += 1
        tok = (e, self.count[e])
        ins.then_inc(self.sems[e], 1)
        self._commit(tok, reads, writes)
        self.ninst += 1
        return tok

    def dma(self, q, out, in_, reads=(), writes=(), **kw):
        self._deps(q, reads, writes)
        keys = self.qkeys[q]
        k = keys[self.qrr[q]]
        self.qrr[q] = (self.qrr[q] + 1) % len(keys)
        if self.count[k] > 0:
            self._wait(q, (k, self.count[k]))
        self.count[k] += 16
        tok = (k, self.count[k])
        self.eng[q].dma_start(out=out, in_=in_, **kw).then_inc(self.sems[k], 16)
        self._commit(tok, reads, writes)
        self.ninst += 1
        return tok

    def barrier(self):
        for e in self.eng:
            for k in self.sems:
                if self.count[k] > 0:
                    self._wait(e, (k, self.count[k]))


class Ctx:
    pass


def _grp(fns):
    ins = None
    for f in fns:
        ins = f()
    return ins


def build_program(n_layers=DEPTH, debug=False):
    nc = bass.Bass("TRN2", target_bir_lowering=False)
    T = Trk(nc)
    c = Ctx()
    c.nc, c.T = nc, T
    S = SEQ

    def din(name, shape, dt=F32):
        return nc.dram_tensor(name, list(shape), dt, kind="ExternalInput").ap()

    def dint(name, shape, dt):
        kind = "ExternalOutput" if debug else "Internal"
        return nc.dram_tensor(name, list(shape), dt, kind=kind).ap()

    def dscr(name, shape, dt):
        return nc.dram_tensor(name, list(shape), dt, kind="Internal").ap()

    xT_in = din("xT_in", [KC, 128, S])
    pT_in = din("pT_in", [DEPTH, 2, 128, S])
    pos_in = din("pos_in", [128, NBLK], I32)
    consts = din("consts", [128, 6, 128])
    invf = din("invf", [128, 96])
    w_in = din("w_in", [DEPTH, D_MODEL, IN_W])
    w_ba = din("w_branch_a", [DEPTH, A_W, D_MODEL])
    w_bb = din("w_branch_b", [DEPTH, B_W, D_MODEL])
    w_out = din("w_out", [DEPTH, D_MODEL, D_MODEL])
    w_fg = din("w_ffn_gate", [DEPTH, D_MODEL, D_FF])
    w_fu = din("w_ffn_up", [DEPTH, D_MODEL, D_FF])
    w_fd = din("w_ffn_down", [DEPTH, D_FF, D_MODEL])
    w_pg = din("w_ple_gate", [DEPTH, D_MODEL, D_MODEL])
    w_pp = din("w_ple_proj", [DEPTH, PLE_DIM, D_MODEL])
    hlb_in = din("hlb_rep", [128, DEPTH, A_W])
    kng_in = din("kn_rep", [128, DEPTH, 2, IDX_DIM])
    hng_in = din("hng_t", [128, DEPTH])
    lnp_in = din("lnp_t", [128, DEPTH, 6, KC])
    out_d = nc.dram_tensor("outT", [KC, 128, S], F32, kind="ExternalOutput").ap()

    wb_in = dscr("wb_in", [DEPTH, D_MODEL, IN_W], BF16)
    wb_ba = dscr("wb_ba", [DEPTH, A_W, D_MODEL], BF16)
    wb_bb = dscr("wb_bb", [DEPTH, B_W, D_MODEL], BF16)
    wb_out = dscr("wb_out", [DEPTH, D_MODEL, D_MODEL], BF16)
    wb_fg = dscr("wb_fg", [DEPTH, D_MODEL, D_FF], BF16)
    wb_fu = dscr("wb_fu", [DEPTH, D_MODEL, D_FF], BF16)
    wb_fd = dscr("wb_fd", [DEPTH, D_FF, D_MODEL], BF16)
    wb_pg = dscr("wb_pg", [DEPTH, D_MODEL, D_MODEL], BF16)
    wb_pp = dscr("wb_pp", [DEPTH, PLE_DIM, D_MODEL], BF16)
    xT_d = dscr("xT_d", [KC, 128, S], BF16)
    xres_d = dscr("xres_d", [KC, 128, S], F32)
    x1T_d = dscr("x1T_d", [KC, 128, S], BF16)
    x1res_d = dscr("x1res_d", [KC, 128, S], F32)
    NPROJ = 7248
    projT = dscr("projT", [S, NPROJ], F32)
    lb_d = dscr("lb_d", [DEPTH, 128, A_W], F32)
    oN_d = dint("oN_d", [8, 128, S], BF16)
    qT_d = dscr("qT_d", [8, 128, S], BF16)
    kT_d = dscr("kT_d", [8, 128, S], BF16)
    v_d = dscr("v_d", [S, B_W], BF16)
    iqT_d = dscr("iqT_d", [8, 128, S], BF16)
    ikT_d = dscr("ikT_d", [128, S], BF16)
    iw_d = dscr("iw_d", [S, IDX_HEADS], F32)
    ybT_d = dint("ybT_d", [8, 128, S], BF16)

    sb = nc.alloc_sbuf_tensor
    cst = sb("cst", [128, 6, 128], F32)
    identb = sb("identb", [128, 128], BF16)
    tri2b = sb("tri2b", [128, 128], BF16)
    onesb = sb("onesb", [128, 128], BF16)
    hng = sb("hng", [128, DEPTH], F32)
    lnp = sb("lnp", [128, DEPTH, 6, KC], F32)
    cosH = sb("cosH", [128, NBLK, 64], F32)
    sinH = sb("sinH", [128, NBLK, 64], F32)
    cosI = sb("cosI", [128, NBLK, 32], F32)
    sinI = sb("sinI", [128, NBLK, 32], F32)
    b_cst = T.buf("cst")
    b_rope = T.buf("rope")
    tri2f = cst[:, 1, :]
    cindf = cst[:, 3, 0:2]
    ps = [nc.alloc_psum_tensor("ps%d" % i, [128, 512], F32) for i in range(8)]
    psb = [ps[6][:].bitcast(BF16), ps[7][:].bitcast(BF16)]
    b_ps = T.bufs(8, "ps")
    b_psb = [b_ps[6], b_ps[7]]

    c.__dict__.update(locals())
    return c


from contextlib import ExitStack

_uid = [0]


def _tiles(es, nc):
    def mk(name, shape, dt):
        _uid[0] += 1
        return es.enter_context(nc.sbuf_tensor("%s_%d" % (name, _uid[0]), list(shape), dt))
    return mk


TWO_PI = float(2 * np.pi)
CW1 = 6.28125
CW2 = 0.0019353071795864769


def conv_list(c, l):
    T = c.T
    out = []
    for (dst, src, rows) in ((c.wb_in, c.w_in, D_MODEL), (c.wb_ba, c.w_ba, A_W), (c.wb_bb, c.w_bb, B_W),
                             (c.wb_out, c.w_out, D_MODEL), (c.wb_fg, c.w_fg, D_MODEL), (c.wb_fu, c.w_fu, D_MODEL),
                             (c.wb_fd, c.w_fd, D_FF), (c.wb_pg, c.w_pg, D_MODEL), (c.wb_pp, c.w_pp, PLE_DIM)):
        for r0 in range(0, rows, 128):
            out.append(lambda dst=dst, src=src, r0=r0: T.dma("pool", dst[l, r0:r0 + 128, :], src[l, r0:r0 + 128, :]))
    return out


def phase_init(c):
    nc, T = c.nc, c.T
    V, A = nc.vector, nc.scalar
    T.dma("sp", c.cst[:], c.consts[:, :, :], writes=[c.b_cst])
    T.dma("sp", c.hng[:], c.hng_in[:, :], writes=[c.b_cst])
    T.dma("sp", c.lnp[:], c.lnp_in[:, :, :, :], writes=[c.b_cst])
    T.op("dve", lambda: V.tensor_copy(c.identb[:], c.cst[:, 0, :]), reads=[c.b_cst], writes=[c.b_cst])
    T.op("dve", lambda: V.tensor_copy(c.tri2b[:], c.cst[:, 1, :]), reads=[c.b_cst], writes=[c.b_cst])
    T.op("dve", lambda: V.tensor_copy(c.onesb[:], c.cst[:, 2, :]), reads=[c.b_cst], writes=[c.b_cst])
    for kc in range(KC):
        T.dma("pool", c.xT_d[kc], c.xT_in[kc])
        T.dma("sp", c.xres_d[kc], c.xT_in[kc])
    for l in range(c.n_layers):
        for fn in conv_list(c, l):
            fn()

    with ExitStack() as es:
        mk = _tiles(es, nc)
        posi = mk("posi", [128, NBLK], I32)
        posf = mk("posf", [128, NBLK], F32)
        ivf = mk("ivf", [128, 96], F32)
        ang = mk("ang", [128, NBLK * 64], F32)
        kf = mk("kf", [128, NBLK * 64], F32)
        ki = mk("ki", [128, NBLK * 64], I32)
        rr = mk("rr", [128, NBLK * 64], F32)
        b_pos, b_ang, b_kf, b_ki, b_rr = T.bufs(5, "rp")
        T.dma("sp", posi[:], c.pos_in[:, :], writes=[b_pos])
        T.dma("sp", ivf[:], c.invf[:, :], writes=[b_pos])
        T.op("dve", lambda: V.tensor_copy(posf[:], posi[:]), reads=[b_pos], writes=[b_pos])
        for (half, off, cosT, sinT) in ((64, 0, c.cosH, c.sinH), (32, 64, c.cosI, c.sinI)):
            n = NBLK * half
            for b in range(NBLK):
                T.op("dve", lambda b=b: V.tensor_scalar(ang[:, b * half:(b + 1) * half], ivf[:, off:off + half],
                                                        posf[:, b:b + 1], None, ALU.mult),
                     reads=[b_pos], writes=[b_ang])
            for (shift, dst) in ((0.0, sinT), (0.25, cosT)):
                T.op("dve", lambda: V.tensor_scalar(kf[:, 0:n], ang[:, 0:n], 1.0 / TWO_PI, shift, ALU.mult, ALU.add),
                     reads=[b_ang], writes=[b_kf])
                T.op("dve", lambda: V.tensor_copy(ki[:, 0:n], kf[:, 0:n]), reads=[b_kf], writes=[b_ki])
                T.op("dve", lambda: V.tensor_copy(kf[:, 0:n], ki[:, 0:n]), reads=[b_ki], writes=[b_kf])
                T.op("dve", lambda: V.scalar_tensor_tensor(rr[:, 0:n], kf[:, 0:n], -CW1, ang[:, 0:n], ALU.mult, ALU.add),
                     reads=[b_kf, b_ang], writes=[b_rr])
                T.op("dve", lambda: V.scalar_tensor_tensor(rr[:, 0:n], kf[:, 0:n], -CW2, rr[:, 0:n], ALU.mult, ALU.add),
                     reads=[b_kf, b_rr], writes=[b_rr])
                if shift:
                    T.op("dve", lambda: V.tensor_scalar(rr[:, 0:n], rr[:, 0:n], float(np.pi / 2), None, ALU.add),
                         reads=[b_rr], writes=[b_rr])
                T.op("dve", lambda: V.tensor_scalar(kf[:, 0:n], rr[:, 0:n], float(np.pi), None, ALU.is_gt),
                     reads=[b_rr], writes=[b_kf])
                T.op("dve", lambda: V.scalar_tensor_tensor(rr[:, 0:n], kf[:, 0:n], -TWO_PI, rr[:, 0:n], ALU.mult, ALU.add),
                     reads=[b_kf, b_rr], writes=[b_rr])
                T.op("dve", lambda: V.tensor_scalar(kf[:, 0:n], rr[:, 0:n], float(-np.pi), None, ALU.is_lt),
                     reads=[b_rr], writes=[b_kf])
                T.op("dve", lambda: V.scalar_tensor_tensor(rr[:, 0:n], kf[:, 0:n], TWO_PI, rr[:, 0:n], ALU.mult, ALU.add),
                     reads=[b_kf, b_rr], writes=[b_rr])
                T.op("dve", lambda: V.tensor_scalar(rr[:, 0:n], rr[:, 0:n], float(np.pi), float(-np.pi), ALU.min, ALU.max),
                     reads=[b_rr], writes=[b_rr])
                T.op("act", lambda dst=dst: A.activation(dst[:].rearrange("p b j -> p (b j)"), rr[:, 0:n], AF.Sin),
                     reads=[b_rr], writes=[c.b_rope])
        hl = mk("hl", [128, DEPTH, A_W], F32)
        ssum = mk("ssum", [128, A_W], F32)
        acc = mk("acc", [128, A_W], F32)
        lbt = mk("lbt", [128, A_W], F32)
        b_hl, b_ss, b_acc, b_lbt = T.bufs(4, "lb")
        T.dma("sp", hl[:], c.hlb_in[:, :, :], writes=[b_hl])
        T.op("act", lambda: A.activation(hl[:].rearrange("p l n -> p (l n)"), hl[:].rearrange("p l n -> p (l n)"), AF.Exp),
             reads=[b_hl], writes=[b_hl])
        T.op("dve", lambda: V.tensor_tensor(ssum[:], hl[:, 0, :], hl[:, 1, :], ALU.add), reads=[b_hl], writes=[b_ss])
        T.op("dve", lambda: V.tensor_tensor(ssum[:], ssum[:], hl[:, 2, :], ALU.add), reads=[b_hl, b_ss], writes=[b_ss])
        T.op("dve", lambda: V.tensor_tensor(ssum[:], ssum[:], hl[:, 3, :], ALU.add), reads=[b_hl, b_ss], writes=[b_ss])
        T.op("dve", lambda: V.reciprocal(ssum[:], ssum[:]), reads=[b_ss], writes=[b_ss])
        T.op("dve", lambda: V.memset(acc[:], 0.0), writes=[b_acc])
        for l in range(DEPTH):
            if l > 0:
                T.op("dve", lambda l=l: V.tensor_tensor(acc[:], acc[:], hl[:, l, :], ALU.add),
                     reads=[b_hl, b_acc], writes=[b_acc])
            T.op("dve", lambda: V.tensor_tensor(lbt[:], acc[:], ssum[:], ALU.mult), reads=[b_acc, b_ss], writes=[b_lbt])
            T.dma("sp", c.lb_d[l], lbt[:], reads=[b_lbt])
        T.barrier()


def phase_proj(c, l):
    nc, T = c.nc, c.T
    PE = nc.tensor
    groups = []
    for (w0, w1, p0) in ((0, 3072, 0), (4096, 8272, 3072)):
        cc = w0
        while cc < w1:
            n = min(512, w1 - cc)
            groups.append((cc, n, p0 + (cc - w0)))
            cc += n
    with ExitStack() as es:
        mk = _tiles(es, nc)
        xh = mk("xh", [128, KC, 2048], BF16)
        wg = [mk("wg%d" % i, [128, KC, 512], BF16) for i in range(2)]
        st = [mk("st%d" % i, [128, 512], F32) for i in range(4)]
        b_xh = T.buf("xh")
        b_wg = T.bufs(2, "wg")
        b_st = T.bufs(4, "st")
        cnt = 0
        for half in range(2):
            T.dma("sp", xh[:], c.xT_d[:, :, half * 2048:(half + 1) * 2048].rearrange("k p t -> p k t"), writes=[b_xh])
            for gi, (w0, n, p0) in enumerate(groups):
                wi = gi % 2
                T.dma("sp", wg[wi][:, :, 0:n],
                      c.wb_in[l, :, w0:w0 + n].rearrange("(k p) n -> p k n", p=128), writes=[b_wg[wi]])
                for tb in range(16):
                    pi = cnt % 4
                    si = cnt % 4
                    cnt += 1
                    T.op("pe", lambda tb=tb, pi=pi, wi=wi, n=n: _grp([
                        (lambda kc=kc: PE.matmul(c.ps[pi][:, 0:n], xh[:, kc, tb * 128:(tb + 1) * 128], wg[wi][:, kc, 0:n],
                                                 start=(kc == 0), stop=(kc == KC - 1))) for kc in range(KC)]),
                         reads=[b_xh, b_wg[wi]], writes=[c.b_ps[pi]])
                    if cnt % 2:
                        T.op("act", lambda pi=pi, si=si, n=n: nc.scalar.copy(st[si][:, 0:n], c.ps[pi][:, 0:n]),
                             reads=[c.b_ps[pi]], writes=[b_st[si]])
                    else:
                        T.op("dve", lambda pi=pi, si=si, n=n: nc.vector.tensor_copy(st[si][:, 0:n], c.ps[pi][:, 0:n]),
                             reads=[c.b_ps[pi]], writes=[b_st[si]])
                    t0 = (half * 16 + tb) * 128
                    T.dma("pool", c.projT[t0:t0 + 128, p0:p0 + n], st[si][:, 0:n], reads=[b_st[si]])
        T.barrier()


def phase_hgrn(c, l):
    nc, T = c.nc, c.T
    PE, V, A, G = nc.tensor, nc.vector, nc.scalar, nc.gpsimd
    ps, b_ps, psb, b_psb = c.ps, c.b_ps, c.psb, c.b_psb
    with ExitStack() as es:
        mk = _tiles(c.shared_es, nc)
        lbt = mk("lbt", [128, A_W], F32)
        oml = mk("oml", [128, A_W], F32)
        aq = [mk("aq%d" % i, [128, A_W], F32) for i in range(2)]
        af = [mk("af%d" % i, [128, A_W], F32) for i in range(2)]
        ai = [mk("ai%d" % i, [128, A_W], F32) for i in range(2)]
        ff = mk("ff", [128, A_W], F32)
        kk = mk("kk", [128, A_W], F32)
        logf = mk("logf", [128, A_W], F32)
        ecum = mk("ecum", [128, A_W], F32)
        encum = mk("encum", [128, A_W], F32)
        qf = mk("qf", [128, A_W], F32)
        qe = mk("qe", [128, A_W], BF16)
        ke = mk("ke", [128, A_W], BF16)
        vt = mk("vt", [128, A_W], BF16)
        qeT = mk("qeT", [128, 8, 128], BF16)
        keT = mk("keT", [128, 8, 128], BF16)
        atm = mk("atm", [128, 8, 128], BF16)
        St = mk("St", [128, 8, 128], F32)
        Sb = mk("Sb", [128, 8, 128], BF16)
        tmpS = [mk("tmpS%d" % i, [128, 128], F32) for i in range(2)]
        elast = mk("elast", [128, 16], F32)
        osq = mk("osq", [128, A_W], BF16)
        rstd = mk("rstd", [128, A_W], F32)
        oNt = [mk("oNt%d" % i, [128, A_W], BF16) for i in range(2)]
        b_lb = T.buf()
        b_aq, b_af, b_ai = T.bufs(2), T.bufs(2), T.bufs(2)
        b_ff, b_kk, b_logf, b_ecum, b_encum, b_qf, b_qe, b_ke, b_vt, b_qeT, b_keT, b_atm = T.bufs(12)
        b_S = T.bufs(8, "S")
        b_Sb = T.bufs(8, "Sb")
        b_tmpS = T.bufs(2)
        b_el, b_osq, b_rstd = T.bufs(3)
        b_oNt = T.bufs(2)
        b_pss = T.bufs(4, "pss")

        T.dma("sp", lbt[:], c.lb_d[l], writes=[b_lb])
        T.op("dve", lambda: V.tensor_scalar(oml[:], lbt[:], -1.0, 1.0, ALU.mult, ALU.add), reads=[b_lb], writes=[b_lb])
        T.op("dve", lambda: V.memset(St[:], 0.0), writes=b_S)
        T.op("dve", lambda: V.memset(Sb[:], 0.0), writes=b_Sb)

        def load(tb):
            i = tb % 2
            t0 = tb * 128
            T.dma("sp", aq[i][:], c.projT[t0:t0 + 128, 0:1024], writes=[b_aq[i]])
            T.dma("sp", af[i][:], c.projT[t0:t0 + 128, 1024:2048], writes=[b_af[i]])
            T.dma("sp", ai[i][:], c.projT[t0:t0 + 128, 2048:3072], writes=[b_ai[i]])

        load(0)
        for tb in range(NBLK):
            i = tb % 2
            if tb + 1 < NBLK:
                load(tb + 1)
            T.op("act", lambda: A.activation(ff[:], af[i][:], AF.Sigmoid), reads=[b_af[i]], writes=[b_ff])
            T.op("dve", lambda: V.tensor_tensor(ff[:], ff[:], oml[:], ALU.mult), reads=[b_ff, b_lb], writes=[b_ff])
            T.op("dve", lambda: V.tensor_tensor(ff[:], ff[:], lbt[:], ALU.add), reads=[b_ff, b_lb], writes=[b_ff])
            T.op("dve", lambda: V.tensor_scalar(kk[:], ff[:], -1.0, 1.0, ALU.mult, ALU.add), reads=[b_ff], writes=[b_kk])
            T.op("act", lambda: A.activation(logf[:], ff[:], AF.Ln), reads=[b_ff], writes=[b_logf])
            for j in range(2):
                T.op("pe", lambda j=j: PE.matmul(ps[j][:], c.tri2f, logf[:, j * 512:(j + 1) * 512], start=True, stop=True),
                     reads=[b_logf, c.b_cst], writes=[b_ps[j]])
            T.op("pe", lambda: _grp([(lambda h=h: PE.matmul(ps[2][:, 2 * h:2 * h + 2], logf[:, h * 128:(h + 1) * 128],
                                                            c.cindf, start=True, stop=True)) for h in range(8)]),
                 reads=[b_logf, c.b_cst], writes=[b_ps[2]])
            T.op("act", lambda: A.activation(elast[:], ps[2][:, 0:16], AF.Exp), reads=[b_ps[2]], writes=[b_el])
            for j in range(2):
                T.op("act", lambda j=j: A.activation(ecum[:, j * 512:(j + 1) * 512], ps[j][:], AF.Exp),
                     reads=[b_ps[j]], writes=[b_ecum])
                T.op("act", lambda j=j: A.activation(encum[:, j * 512:(j + 1) * 512], ps[j][:], AF.Exp, scale=-1.0),
                     reads=[b_ps[j]], writes=[b_encum])
            T.op("act", lambda: A.activation(qf[:], aq[i][:], AF.Silu), reads=[b_aq[i]], writes=[b_qf])
            T.op("dve", lambda: V.tensor_tensor(qe[:], qf[:], ecum[:], ALU.mult), reads=[b_qf, b_ecum], writes=[b_qe])
            T.op("dve", lambda: V.tensor_tensor(ke[:], kk[:], encum[:], ALU.mult), reads=[b_kk, b_encum], writes=[b_ke])
            T.op("pool", lambda: G.tensor_copy(vt[:], ai[i][:]), reads=[b_ai[i]], writes=[b_vt])
            T.op("pe", lambda: _grp([(lambda h=h: PE.transpose(psb[0][:, h * 128:(h + 1) * 128], qe[:, h * 128:(h + 1) * 128],
                                                               c.identb[:])) for h in range(8)]),
                 reads=[b_qe, c.b_cst], writes=[b_psb[0]])
            T.op("pe", lambda: _grp([(lambda h=h: PE.transpose(psb[1][:, h * 128:(h + 1) * 128], ke[:, h * 128:(h + 1) * 128],
                                                               c.identb[:])) for h in range(8)]),
                 reads=[b_ke, c.b_cst], writes=[b_psb[1]])
            T.op("act", lambda: A.copy(qeT[:].rearrange("p h t -> p (h t)"), psb[0][:]), reads=[b_psb[0]], writes=[b_qeT])
            T.op("dve", lambda: V.tensor_copy(keT[:].rearrange("p h t -> p (h t)"), psb[1][:]), reads=[b_psb[1]], writes=[b_keT])
            for g in range(2):
                T.op("pe", lambda g=g: _grp([(lambda h=h: PE.matmul(ps[3 + g][:, (h % 4) * 128:(h % 4 + 1) * 128],
                                                                    keT[:, h, :], qeT[:, h, :], start=True, stop=True))
                                             for h in range(4 * g, 4 * g + 4)]),
                     reads=[b_keT, b_qeT], writes=[b_ps[3 + g]])
                T.op("dve", lambda g=g: V.tensor_tensor(atm[:, 4 * g:4 * g + 4, :],
                                                        ps[3 + g][:].rearrange("p (h t) -> p h t", h=4),
                                                        c.tri2b[:].unsqueeze(1).to_broadcast([128, 4, 128]), ALU.mult),
                     reads=[b_ps[3 + g], c.b_cst], writes=[b_atm])
            for h in range(8):
                pso = ps[h // 4]
                oc = (h % 4) * 128
                for cch in range(2):
                    r0 = 64 * cch
                    T.op("pe", lambda h=h, r0=r0, pso=pso, oc=oc: _grp([
                        lambda: PE.matmul(pso[:, oc + r0:oc + r0 + 64], vt[r0:r0 + 64, h * 128:(h + 1) * 128],
                                          atm[r0:r0 + 64, h, r0:r0 + 64], start=True, stop=False),
                        lambda: PE.matmul(pso[:, oc + r0:oc + r0 + 64], Sb[:, h, :], qeT[:, h, r0:r0 + 64],
                                          start=False, stop=True)]),
                         reads=[b_vt, b_atm, b_Sb[h], b_qeT, b_ecum, b_encum], writes=[b_ps[h // 4]])
                    sl = (2 * h + cch) % 4
                    T.op("pe", lambda h=h, r0=r0, sl=sl: PE.matmul(ps[2 + sl][:, 0:128],
                                                                   ke[r0:r0 + 64, h * 128:(h + 1) * 128],
                                                                   vt[r0:r0 + 64, h * 128:(h + 1) * 128], start=True, stop=True),
                         reads=[b_ke, b_vt], writes=[b_ps[2 + sl]])
                    ti = (2 * h + cch) % 2
                    T.op("dve", lambda h=h, sl=sl, ti=ti: V.tensor_tensor(tmpS[ti][:], ps[2 + sl][:, 0:128],
                                                                          St[:, h, :], ALU.add),
                         reads=[b_ps[2 + sl], b_S[h]], writes=[b_tmpS[ti]])
                    T.op("dve", lambda h=h, ti=ti, cch=cch: V.tensor_scalar(St[:, h, :], tmpS[ti][:],
                                                                            elast[:, 2 * h + cch:2 * h + cch + 1], None, ALU.mult),
                         reads=[b_tmpS[ti], b_el], writes=[b_S[h]])
                    T.op("act", lambda h=h: A.copy(Sb[:, h, :], St[:, h, :]), reads=[b_S[h]], writes=[b_Sb[h]])
            for j in range(2):
                T.op("act", lambda j=j: A.activation(osq[:, j * 512:(j + 1) * 512], ps[j][:], AF.Square),
                     reads=[b_ps[j]], writes=[b_osq])
                T.op("pe", lambda j=j: PE.matmul(ps[3 + j][:], c.onesb[:], osq[:, j * 512:(j + 1) * 512], start=True, stop=True),
                     reads=[b_osq, c.b_cst], writes=[b_ps[3 + j]])
                T.op("act", lambda j=j: A.activation(rstd[:, j * 512:(j + 1) * 512], ps[3 + j][:], AF.Sqrt,
                                                     bias=RMS_EPS, scale=1.0 / 128.0),
                     reads=[b_ps[3 + j]], writes=[b_rstd])
            T.op("dve", lambda: V.reciprocal(rstd[:], rstd[:]), reads=[b_rstd], writes=[b_rstd])
            for j in range(2):
                T.op("dve", lambda j=j: V.scalar_tensor_tensor(oNt[i][:, j * 512:(j + 1) * 512], ps[j][:], c.hng[:, l:l + 1],
                                                               rstd[:, j * 512:(j + 1) * 512], ALU.mult, ALU.mult),
                     reads=[b_ps[j], b_rstd, c.b_cst], writes=[b_oNt[i]])
            T.dma("pool", c.oN_d[:, :, tb * 128:(tb + 1) * 128].rearrange("h p t -> p h t"),
                  oNt[i][:].rearrange("p (h t) -> p h t", h=8), reads=[b_oNt[i]])
            yield tb
        T.barrier()


def _rope(T, eng_name, eng, out_v, x_v, cos_b, sin_b, tmp, b_in, b_out, b_tmp, b_rope):
    x1, x2 = x_v[:, :, 0, :], x_v[:, :, 1, :]
    o1, o2 = out_v[:, :, 0, :], out_v[:, :, 1, :]
    t1, t2 = tmp
    T.op(eng_name, lambda: eng.tensor_tensor(t1, x1, cos_b, ALU.mult), reads=[b_in, b_rope], writes=[b_tmp[0]])
    T.op(eng_name, lambda: eng.tensor_tensor(t2, x2, sin_b, ALU.mult), reads=[b_in, b_rope], writes=[b_tmp[1]])
    T.op(eng_name, lambda: eng.tensor_tensor(o1, t1, t2, ALU.subtract), reads=[b_tmp[0], b_tmp[1]], writes=[b_out])
    T.op(eng_name, lambda: eng.tensor_tensor(t1, x1, sin_b, ALU.mult), reads=[b_in, b_rope], writes=[b_tmp[0]])
    T.op(eng_name, lambda: eng.tensor_tensor(t2, x2, cos_b, ALU.mult), reads=[b_in, b_rope], writes=[b_tmp[1]])
    T.op(eng_name, lambda: eng.tensor_tensor(o2, t1, t2, ALU.add), reads=[b_tmp[0], b_tmp[1]], writes=[b_out])


def phase_dsa_prep(c, l):
    nc, T = c.nc, c.T
    PE, V, A, G = nc.tensor, nc.vector, nc.scalar, nc.gpsimd
    ps, b_ps, psb, b_psb = c.ps, c.b_ps, c.psb, c.b_psb
    with ExitStack() as es:
        mk = _tiles(c.shared_es, nc)
        bq = [mk("bq%d" % i, [128, 1024], F32) for i in range(2)]
        bk = [mk("bk%d" % i, [128, 1024], F32) for i in range(2)]
        bv = [mk("bv%d" % i, [128, 1024], F32) for i in range(2)]
        iq = [mk("iq%d" % i, [128, 1024], F32) for i in range(2)]
        ikw = [mk("ikw%d" % i, [128, 80], F32) for i in range(2)]
        kn = mk("kn", [128, 2, 64], F32)
        tq = [mk("tq%d" % i, [128, 512], F32) for i in range(2)]
        tk = [mk("tk%d" % i, [128, 512], F32) for i in range(2)]
        qr = mk("qr", [128, 1024], BF16)
        kr = mk("kr", [128, 1024], BF16)
        iqr = mk("iqr", [128, 1024], BF16)
        vb = [mk("vb%d" % i, [128, 1024], BF16) for i in range(2)]
        qTt = [mk("qTt%d" % i, [128, 8, 128], BF16) for i in range(2)]
        kTt = [mk("kTt%d" % i, [128, 8, 128], BF16) for i in range(2)]
        iqTt = [mk("iqTt%d" % i, [128, 8, 128], BF16) for i in range(2)]
        ikT = [mk("ikT%d" % i, [128, 128], BF16) for i in range(2)]
        st1 = mk("st1", [128, 8], F32)
        xc = mk("xc", [128, 64], F32)
        sq = mk("sq", [128, 64], F32)
        t3 = [mk("t3%d" % i, [128, 32], F32) for i in range(2)]
        ikr = mk("ikr", [128, 128], BF16)
        iwt = [mk("iwt%d" % i, [128, 16], F32) for i in range(2)]
        b_bq, b_bk, b_bv, b_iq, b_ikw = T.bufs(2), T.bufs(2), T.bufs(2), T.bufs(2), T.bufs(2)
        b_kn, b_qr, b_kr, b_iqr, b_st1, b_xc, b_sq, b_ikr = T.bufs(8)
        b_tq, b_tk, b_t3 = T.bufs(2), T.bufs(2), T.bufs(2)
        b_vb, b_qTt, b_kTt, b_iqTt, b_ikT, b_iwt = T.bufs(2), T.bufs(2), T.bufs(2), T.bufs(2), T.bufs(2), T.bufs(2)
        T.dma("sp", kn[:], c.kng_in[:, l, :, :], writes=[b_kn])

        def load(tb):
            i = tb % 2
            t0 = tb * 128
            T.dma("sp", bq[i][:], c.projT[t0:t0 + 128, 3072:4096], writes=[b_bq[i]])
            T.dma("sp", bk[i][:], c.projT[t0:t0 + 128, 4096:5120], writes=[b_bk[i]])
            T.dma("sp", bv[i][:], c.projT[t0:t0 + 128, 5120:6144], writes=[b_bv[i]])
            T.dma("sp", iq[i][:], c.projT[t0:t0 + 128, 6144:7168], writes=[b_iq[i]])
            T.dma("sp", ikw[i][:], c.projT[t0:t0 + 128, 7168:7248], writes=[b_ikw[i]])

        load(0)
        for tb in range(NBLK):
            i = tb % 2
            tsl = slice(tb * 128, (tb + 1) * 128)
            if tb + 1 < NBLK:
                load(tb + 1)
            cosb = c.cosH[:, tb, :].unsqueeze(1).to_broadcast([128, 8, 64])
            sinb = c.sinH[:, tb, :].unsqueeze(1).to_broadcast([128, 8, 64])
            cosib = c.cosI[:, tb, :].unsqueeze(1).to_broadcast([128, 16, 32])
            sinib = c.sinI[:, tb, :].unsqueeze(1).to_broadcast([128, 16, 32])
            v4 = lambda t: t[:].rearrange("p (h two j) -> p h two j", two=2, j=64)
            v4i = lambda t: t[:].rearrange("p (h two j) -> p h two j", two=2, j=32)
            tqv = [t[:].rearrange("p (h j) -> p h j", j=64) for t in tq]
            tkv = [t[:].rearrange("p (h j) -> p h j", j=64) for t in tk]
            tqiv = [t[:].rearrange("p (h j) -> p h j", j=32) for t in tq]
            _rope(T, "dve", V, v4(qr), v4(bq[i]), cosb, sinb, tqv, b_bq[i], b_qr, b_tq, c.b_rope)
            _rope(T, "pool", G, v4(kr), v4(bk[i]), cosb, sinb, tkv, b_bk[i], b_kr, b_tk, c.b_rope)
            _rope(T, "dve", V, v4i(iqr), v4i(iq[i]), cosib, sinib, tqiv, b_iq[i], b_iqr, b_tq, c.b_rope)
            T.op("act", lambda: A.copy(vb[i][:], bv[i][:]), reads=[b_bv[i]], writes=[b_vb[i]])
            T.dma("pool", c.v_d[tsl, :], vb[i][:], reads=[b_vb[i]])
            for (src, bsrc, pi, dst, bdst, dd) in ((qr, b_qr, 0, qTt[i], b_qTt[i], c.qT_d), (kr, b_kr, 1, kTt[i], b_kTt[i], c.kT_d),
                                                   (iqr, b_iqr, 0, iqTt[i], b_iqTt[i], c.iqT_d)):
                T.op("pe", lambda src=src, pi=pi: _grp([(lambda h=h: PE.transpose(psb[pi][:, h * 128:(h + 1) * 128],
                                                                                  src[:, h * 128:(h + 1) * 128], c.identb[:]))
                                                        for h in range(8)]),
                     reads=[bsrc, c.b_cst], writes=[b_psb[pi]])
                T.op("act", lambda dst=dst, pi=pi: A.copy(dst[:].rearrange("p h t -> p (h t)"), psb[pi][:]),
                     reads=[b_psb[pi]], writes=[bdst])
                T.dma("pool", dd[:, :, tsl].rearrange("h p t -> p h t"), dst[:], reads=[bdst])
            ik = ikw[i][:, 0:64]
            T.op("dve", lambda: V.tensor_reduce(st1[:, 0:1], ik, AX.X, ALU.add), reads=[b_ikw[i]], writes=[b_st1])
            T.op("dve", lambda: V.tensor_scalar(st1[:, 1:2], st1[:, 0:1], -1.0 / 64.0, None, ALU.mult), reads=[b_st1], writes=[b_st1])
            T.op("dve", lambda: V.tensor_scalar(xc[:], ik, st1[:, 1:2], None, ALU.add), reads=[b_ikw[i], b_st1], writes=[b_xc])
            T.op("dve", lambda: V.tensor_tensor(sq[:], xc[:], xc[:], ALU.mult), reads=[b_xc], writes=[b_sq])
            T.op("dve", lambda: V.tensor_reduce(st1[:, 2:3], sq[:], AX.X, ALU.add), reads=[b_sq], writes=[b_st1])
            T.op("act", lambda: A.activation(st1[:, 3:4], st1[:, 2:3], AF.Sqrt, bias=LN_EPS, scale=1.0 / 64.0),
                 reads=[b_st1], writes=[b_st1])
            T.op("dve", lambda: V.reciprocal(st1[:, 4:5], st1[:, 3:4]), reads=[b_st1], writes=[b_st1])
            T.op("dve", lambda: V.tensor_scalar(xc[:], xc[:], st1[:, 4:5], None, ALU.mult), reads=[b_xc, b_st1], writes=[b_xc])
            T.op("dve", lambda: V.tensor_tensor(xc[:], xc[:], kn[:, 0, :], ALU.mult), reads=[b_xc, b_kn], writes=[b_xc])
            T.op("dve", lambda: V.tensor_tensor(xc[:], xc[:], kn[:, 1, :], ALU.add), reads=[b_xc, b_kn], writes=[b_xc])
            ci, si = c.cosI[:, tb, :], c.sinI[:, tb, :]
            x1, x2 = xc[:, 0:32], xc[:, 32:64]
            T.op("dve", lambda: V.tensor_tensor(t3[0][:], x1, ci, ALU.mult), reads=[b_xc, c.b_rope], writes=[b_t3[0]])
            T.op("dve", lambda: V.tensor_tensor(t3[1][:], x2, si, ALU.mult), reads=[b_xc, c.b_rope], writes=[b_t3[1]])
            T.op("dve", lambda: V.tensor_tensor(ikr[:, 0:32], t3[0][:], t3[1][:], ALU.subtract), reads=b_t3, writes=[b_ikr])
            T.op("dve", lambda: V.tensor_tensor(ikr[:, 64:96], t3[0][:], t3[1][:], ALU.subtract), reads=b_t3, writes=[b_ikr])
            T.op("dve", lambda: V.tensor_tensor(t3[0][:], x1, si, ALU.mult), reads=[b_xc, c.b_rope], writes=[b_t3[0]])
            T.op("dve", lambda: V.tensor_tensor(t3[1][:], x2, ci, ALU.mult), reads=[b_xc, c.b_rope], writes=[b_t3[1]])
            T.op("dve", lambda: V.tensor_tensor(ikr[:, 32:64], t3[0][:], t3[1][:], ALU.add), reads=b_t3, writes=[b_ikr])
            T.op("dve", lambda: V.tensor_tensor(ikr[:, 96:128], t3[0][:], t3[1][:], ALU.add), reads=b_t3, writes=[b_ikr])
            T.op("pe", lambda: PE.transpose(psb[1][:, 0:128], ikr[:], c.identb[:]), reads=[b_ikr, c.b_cst], writes=[b_psb[1]])
            T.op("act", lambda: A.copy(ikT[i][:], psb[1][:, 0:128]), reads=[b_psb[1]], writes=[b_ikT[i]])
            T.dma("pool", c.ikT_d[:, tsl], ikT[i][:], reads=[b_ikT[i]])
            T.op("dve", lambda: V.tensor_scalar(iwt[i][:], ikw[i][:, 64:80], IDX_SCALE, None, ALU.mult),
                 reads=[b_ikw[i]], writes=[b_iwt[i]])
            T.dma("pool", c.iw_d[tsl, :], iwt[i][:], reads=[b_iwt[i]])
            yield tb
        T.barrier()


def phase_dsa(c, l):
    nc, T = c.nc, c.T
    PE, V, A, G = nc.tensor, nc.vector, nc.scalar, nc.gpsimd
    ps, b_ps = c.ps, c.b_ps
    psT, b_psT = c.psb[1], c.b_ps[7]
    with ExitStack() as es:
        mk = _tiles(es, nc)
        ikT2 = mk("ikT2", [128, SEQ], BF16)
        iqTt = [mk("iqTt%d" % i, [128, 8, 128], BF16) for i in range(2)]
        qTt = [mk("qTt%d" % i, [128, 8, 128], BF16) for i in range(2)]
        iwt = [mk("iwt%d" % i, [128, 16], F32) for i in range(2)]
        diag = [mk("diag%d" % i, [128, 16, 128], BF16) for i in range(2)]
        sc = [mk("sc%d" % i, [128, SEQ], F32) for i in range(2)]
        work = mk("work", [128, SEQ], F32)
        rl = [mk("rl%d" % i, [128, 512], BF16) for i in range(2)]
        m8 = mk("m8", [128, 8], F32)
        maskt = mk("maskt", [128, SEQ], BF16)
        maskT = [mk("maskT%d" % i, [128, NBLK, 128], BF16) for i in range(2)]
        kTh = [mk("kTh%d" % i, [128, SEQ], BF16) for i in range(2)]
        vh = [mk("vh%d" % i, [128, NBLK, 128], BF16) for i in range(2)]
        Et = [mk("Et%d" % i, [128, 512], BF16) for i in range(2)]
        PM = [mk("PM%d" % i, [128, 512], BF16) for i in range(2)]
        rec = [mk("rec%d" % i, [128, 128], F32) for i in range(2)]
        nd = [mk("nd%d" % i, [128, 128], F32) for i in range(2)]
        b_nd = T.bufs(2)
        ybt = [mk("ybt%d" % i, [128, 8, 128], BF16) for i in range(2)]
        b_ik = T.buf()
        b_iqT, b_qT, b_iw, b_rl, b_diag, b_sc, b_maskT = (T.bufs(2), T.bufs(2), T.bufs(2), T.bufs(2), T.bufs(2),
                                                          T.bufs(2), T.bufs(2))
        b_work, b_m8, b_maskt = T.bufs(3)
        b_kTh, b_vh, b_Et, b_PM, b_rec, b_ybt = T.bufs(2), T.bufs(2), T.bufs(2), T.bufs(2), T.bufs(2), T.bufs(2)
        T.dma("sp", ikT2[:], c.ikT_d[:, :], writes=[b_ik])
        st = {"hcnt": 0, "gcnt": 0}

        def score(qb):
            i = qb % 2
            L = 128 * (qb + 1)
            tsl = slice(qb * 128, (qb + 1) * 128)
            T.dma("sp", iqTt[i][:], c.iqT_d[:, :, tsl].rearrange("h p t -> p h t"), writes=[b_iqT[i]])
            T.dma("sp", iwt[i][:], c.iw_d[tsl, :], writes=[b_iw[i]])
            for hi in range(IDX_HEADS):
                T.op("pool", lambda hi=hi: G.tensor_scalar(diag[i][:, hi, :], c.identb[:], iwt[i][:, hi:hi + 1], None, ALU.mult),
                     reads=[b_iw[i], c.b_cst], writes=[b_diag[i]])
            nst = (L + 511) // 512
            for s_t in range(nst):
                s0 = s_t * 512
                n = min(512, L - s0)

                def idx(hi):
                    pair, half, pj = hi // 2, hi % 2, hi % 2
                    T.op("pe", lambda: PE.matmul(ps[pj][:, 0:n], iqTt[i][half * 64:(half + 1) * 64, pair, :],
                                                 ikT2[half * 64:(half + 1) * 64, s0:s0 + n], start=True, stop=True),
                         reads=[b_iqT[i], b_ik], writes=[b_ps[pj]])
                idx(0)
                for hi in range(IDX_HEADS):
                    pj = hi % 2
                    if hi + 1 < IDX_HEADS:
                        idx(hi + 1)
                    T.op("act", lambda pj=pj: A.activation(rl[pj][:, 0:n], ps[pj][:, 0:n], AF.Relu),
                         reads=[b_ps[pj]], writes=[b_rl[pj]])
                    T.op("pe", lambda pj=pj, hi=hi: PE.matmul(ps[2][:, 0:n], diag[i][:, hi, :], rl[pj][:, 0:n],
                                                              start=(hi == 0), stop=(hi == IDX_HEADS - 1)),
                         reads=[b_diag[i], b_rl[pj]], writes=[b_ps[2]])
                T.op("act", lambda: A.copy(sc[i][:, s0:s0 + n], ps[2][:, 0:n]), reads=[b_ps[2]], writes=[b_sc[i]])
            T.op("pool", lambda: G.memset(sc[i][0:64, L - 64:L], NEG), reads=[b_sc[i]], writes=[b_sc[i]])

        def topk_rounds(qb):
            i = qb % 2
            L = 128 * (qb + 1)
            if qb < 2:
                return
            nr = TOPK // 8
            for r in range(nr):
                src = sc[i] if r == 0 else work
                bsrc = b_sc[i] if r == 0 else b_work
                T.op("dve", lambda src=src: V.max(out=m8[:], in_=src[:, 0:L]), reads=[bsrc], writes=[b_m8])
                if r < nr - 1:
                    T.op("dve", lambda src=src: V.match_replace(out=work[:, 0:L], in_to_replace=m8[:], in_values=src[:, 0:L],
                                                                imm_value=-3.0e38),
                         reads=[bsrc, b_m8], writes=[b_work])
                yield r

        def mask(qb):
            i = qb % 2
            L = 128 * (qb + 1)
            nkb = qb + 1
            if qb >= 2:
                T.op("dve", lambda: V.tensor_scalar(maskt[:, 0:L], sc[i][:, 0:L], m8[:, 7:8], None, ALU.is_ge),
                     reads=[b_sc[i], b_m8], writes=[b_maskt])
            else:
                T.op("dve", lambda: V.tensor_scalar(maskt[:, 0:L], sc[i][:, 0:L], -1.0e29, None, ALU.is_ge),
                     reads=[b_sc[i]], writes=[b_maskt])
            for g0 in range(0, nkb, 8):
                nb = min(8, nkb - g0)
                T.op("pe", lambda g0=g0, nb=nb: _grp([(lambda j=j: PE.transpose(
                    psT[:, j * 128:(j + 1) * 128], maskt[:, (g0 + j) * 128:(g0 + j + 1) * 128], c.identb[:])) for j in range(nb)]),
                     reads=[b_maskt, c.b_cst], writes=[b_psT])
                T.op("act", lambda g0=g0, nb=nb: A.copy(maskT[i][:, g0:g0 + nb, :].rearrange("p b t -> p (b t)"),
                                                        psT[:, 0:nb * 128]),
                     reads=[b_psT], writes=[b_maskT[i]])

        def attention(qb):
            i = qb % 2
            L = 128 * (qb + 1)
            nkb = qb + 1
            tsl = slice(qb * 128, (qb + 1) * 128)
            deferred = []
            T.dma("sp", qTt[i][:], c.qT_d[:, :, tsl].rearrange("h p t -> p h t"), writes=[b_qT[i]])
            for h in range(8):
                hi2 = st["hcnt"] % 2
                bnk = 4 + (st["hcnt"] % 2)
                ri = st["hcnt"] % 2
                st["hcnt"] += 1
                T.dma("sp", kTh[hi2][:, 0:L], c.kT_d[h, :, 0:L], writes=[b_kTh[hi2]])
                T.dma("sp", vh[hi2][:, 0:nkb, :], c.v_d[0:L, h * 128:(h + 1) * 128].rearrange("(b p) d -> p b d", p=128),
                      writes=[b_vh[hi2]])
                num = ps[bnk][:, 0:128]
                den = ps[bnk][:, 128:256]
                groups = [(g0, min(4, nkb - g0)) for g0 in range(0, nkb, 4)]
                pend = None

                def pv(g0, nb, gi):
                    fns = []
                    for j in range(nb):
                        kb = g0 + j
                        fns.append(lambda j=j, kb=kb: PE.matmul(num, vh[hi2][:, kb, :], PM[gi][:, j * 128:(j + 1) * 128],
                                                                start=(kb == 0), stop=(kb == nkb - 1), skip_group_check=True))
                        fns.append(lambda j=j, kb=kb: PE.matmul(den, c.onesb[:], PM[gi][:, j * 128:(j + 1) * 128],
                                                                start=False, stop=(kb == nkb - 1), skip_group_check=True))
                    T.op("pe", lambda: _grp(fns), reads=[b_vh[hi2], b_PM[gi], c.b_cst], writes=[b_ps[bnk]])

                for (g0, nb) in groups:
                    gi = st["gcnt"] % 2
                    st["gcnt"] += 1
                    T.op("pe", lambda g0=g0, nb=nb, gi=gi: _grp([(lambda j=j: PE.matmul(
                        ps[3 + 3 * gi][:, j * 128:(j + 1) * 128], kTh[hi2][:, (g0 + j) * 128:(g0 + j + 1) * 128], qTt[i][:, h, :],
                        start=True, stop=True)) for j in range(nb)]),
                         reads=[b_kTh[hi2], b_qT[i]], writes=[b_ps[3 + 3 * gi]])
                    T.op("act", lambda nb=nb, gi=gi: A.activation(Et[gi][:, 0:nb * 128], ps[3 + 3 * gi][:, 0:nb * 128], AF.Exp,
                                                                  scale=ATTN_SCALE),
                         reads=[b_ps[3 + 3 * gi]], writes=[b_Et[gi]])
                    T.op("pool", lambda g0=g0, nb=nb, gi=gi: G.tensor_tensor(
                        PM[gi][:, 0:nb * 128], Et[gi][:, 0:nb * 128],
                        maskT[i][:, g0:g0 + nb, :].rearrange("p b t -> p (b t)"), ALU.mult),
                         reads=[b_Et[gi], b_maskT[i]], writes=[b_PM[gi]])
                    if pend is not None:
                        pv(*pend)
                    pend = (g0, nb, gi)
                pv(*pend)

                T.op("act", lambda: A.copy(nd[ri][:], num), reads=[b_ps[bnk]], writes=[b_nd[ri]])
                T.op("act", lambda: A.activation(rec[ri][:], den, AF.Ln), reads=[b_ps[bnk]], writes=[b_rec[ri]])
                T.op("act", lambda: A.activation(rec[ri][:], rec[ri][:], AF.Exp, scale=-1.0), reads=[b_rec[ri]], writes=[b_rec[ri]])
                T.op("pool", lambda: G.tensor_tensor(ybt[i][:, h, :], nd[ri][:], rec[ri][:], ALU.mult),
                     reads=[b_nd[ri], b_rec[ri]], writes=[b_ybt[i]])
            T.dma("pool", c.ybT_d[:, :, tsl].rearrange("h p t -> p h t"), ybt[i][:], reads=[b_ybt[i]])
            return deferred

        bg = []
        per_it = (len(bg) + NBLK - 1) // NBLK
        score(0)
        for it in range(0, NBLK + 1):
            qs, qt, qa = it + 1, it, it - 1
            for _ in range(per_it):
                if bg:
                    bg.pop(0)()
            tk = topk_rounds(qt) if qt < NBLK else iter(())
            for _ in range(3):
                next(tk, None)
            deferred = attention(qa) if qa >= 0 else []
            if qs < NBLK:
                score(qs)
            cnt = 0
            for _ in tk:
                cnt += 1
                if cnt % 3 == 0 and deferred:
                    deferred.pop(0)()
            while deferred:
                deferred.pop(0)()
            if qt < NBLK:
                mask(qt)
        T.barrier()


def _ln_stats(c, z, b_z, kc, zb, b_zb):
    nc, T = c.nc, c.T
    PE, A = nc.tensor, nc.scalar
    ps, b_ps = c.ps, c.b_ps
    zi = kc % 2
    T.op("act", lambda: A.copy(zb[zi][:, 0, :], z[:, kc, :]), reads=[b_z[kc]], writes=[b_zb[zi]])
    T.op("act", lambda: A.activation(zb[zi][:, 1, :], z[:, kc, :], AF.Square), reads=[b_z[kc]], writes=[b_zb[zi]])
    T.op("pe", lambda: PE.matmul(ps[4][:], c.onesb[:], zb[zi][:, 0, :], start=(kc == 0), stop=(kc == KC - 1)),
         reads=[b_zb[zi], c.b_cst], writes=[b_ps[4]])
    T.op("pe", lambda: PE.matmul(ps[5][:], c.onesb[:], zb[zi][:, 1, :], start=(kc == 0), stop=(kc == KC - 1)),
         reads=[b_zb[zi], c.b_cst], writes=[b_ps[5]])


def _ln_finish(c, z, b_z, l, jg, xb, b_xb, scratch, b_scr):
    nc, T = c.nc, c.T
    V, A = nc.vector, nc.scalar
    ps, b_ps = c.ps, c.b_ps
    mean, rstd, msq = scratch
    T.op("dve", lambda: V.tensor_scalar(mean[:], ps[4][:], 1.0 / D_MODEL, None, ALU.mult), reads=[b_ps[4]], writes=[b_scr[0]])
    T.op("dve", lambda: V.tensor_tensor(msq[:], mean[:], mean[:], ALU.mult), reads=[b_scr[0]], writes=[b_scr[2]])
    T.op("dve", lambda: V.scalar_tensor_tensor(rstd[:], ps[5][:], 1.0 / D_MODEL, msq[:], ALU.mult, ALU.subtract),
         reads=[b_ps[5], b_scr[2]], writes=[b_scr[1]])
    T.op("act", lambda: A.activation(rstd[:], rstd[:], AF.Sqrt, bias=LN_EPS, scale=1.0), reads=[b_scr[1]], writes=[b_scr[1]])
    T.op("dve", lambda: V.reciprocal(rstd[:], rstd[:]), reads=[b_scr[1]], writes=[b_scr[1]])
    for kc in range(KC):
        T.op("dve", lambda kc=kc: V.tensor_tensor(z[:, kc, :], z[:, kc, :], mean[:], ALU.subtract),
             reads=[b_z[kc], b_scr[0]], writes=[b_z[kc]])
        T.op("dve", lambda kc=kc: V.tensor_tensor(z[:, kc, :], z[:, kc, :], rstd[:], ALU.mult),
             reads=[b_z[kc], b_scr[1]], writes=[b_z[kc]])
        T.op("act", lambda kc=kc: A.activation(z[:, kc, :], z[:, kc, :], AF.Identity, bias=c.lnp[:, l, jg + 1, kc:kc + 1],
                                               scale=c.lnp[:, l, jg, kc:kc + 1]),
             reads=[b_z[kc], c.b_cst], writes=[b_z[kc]])
        T.op("act", lambda kc=kc: A.copy(xb[:, kc, :], z[:, kc, :]), reads=[b_z[kc]], writes=[b_xb[kc]])


def _wload(c, dst, bdst, src2d, ncols):
    c.T.dma("sp", dst[:, :, 0:ncols], src2d.rearrange("(k p) n -> p k n", p=128), writes=[bdst])


def phase_stageA(c, l):
    nc, T = c.nc, c.T
    PE, V, A, G = nc.tensor, nc.vector, nc.scalar, nc.gpsimd
    ps, b_ps = c.ps, c.b_ps
    WC = 256
    with ExitStack() as es:
        mk = _tiles(es, nc)
        xT = mk("xT", [128, KC, TT], BF16)
        oN = mk("oN", [128, 8, TT], BF16)
        yb = mk("yb", [128, 8, TT], BF16)
        ya = mk("ya", [128, 8, TT], BF16)
        mg = mk("mg", [128, KC, TT], BF16)
        z = mk("z", [128, KC, TT], F32)
        xb = mk("xb", [128, KC, TT], BF16)
        xr = [mk("xr%d" % i, [128, TT], F32) for i in range(2)]
        w16 = [mk("w16_%d" % i, [128, KC, WC], BF16) for i in range(4)]
        w8 = [mk("w8_%d" % i, [128, 8, WC], BF16) for i in range(4)]
        sg = [mk("sg%d" % i, [128, TT], F32) for i in range(2)]
        tm = [mk("tm%d" % i, [128, TT], F32) for i in range(2)]
        scratch = [mk("lns%d" % i, [128, TT], F32) for i in range(3)]
        zb = [mk("zb%d" % i, [128, 2, TT], BF16) for i in range(2)]
        b_xT, b_oN, b_yb = T.buf(), T.buf(), T.buf()
        b_ya = T.bufs(8)
        b_mg = T.bufs(KC)
        b_z = T.bufs(KC)
        b_xb = T.bufs(KC)
        b_xr, b_w16, b_w8, b_sg, b_tm, b_scr, b_zb = T.bufs(2), T.bufs(4), T.bufs(4), T.bufs(2), T.bufs(2), T.bufs(3), T.bufs(2)
        cnt = 0
        for tt in range(SEQ // TT):
            tsl = slice(tt * TT, (tt + 1) * TT)
            T.dma("sp", xT[:], c.xT_d[:, :, tsl].rearrange("k p t -> p k t"), writes=[b_xT])
            T.dma("sp", oN[:], c.oN_d[:, :, tsl].rearrange("k p t -> p k t"), writes=[b_oN])
            T.dma("sp", yb[:], c.ybT_d[:, :, tsl].rearrange("k p t -> p k t"), writes=[b_yb])
            for g in range(A_W // WC):
                wi = g % 2
                _wload(c, w16[wi], b_w16[wi], c.wb_in[l, :, 3072 + g * WC:3072 + (g + 1) * WC], WC)
                for j in range(WC // 128):
                    h = g * (WC // 128) + j
                    pi = cnt % 4
                    cnt += 1
                    T.op("pe", lambda wi=wi, j=j, pi=pi: _grp([(lambda kc=kc: PE.matmul(
                        ps[pi][:], w16[wi][:, kc, j * 128:(j + 1) * 128], xT[:, kc, :], start=(kc == 0), stop=(kc == KC - 1)))
                        for kc in range(KC)]), reads=[b_w16[wi], b_xT], writes=[b_ps[pi]])
                    si = cnt % 2
                    T.op("act", lambda pi=pi, si=si: A.activation(sg[si][:], ps[pi][:], AF.Silu), reads=[b_ps[pi]], writes=[b_sg[si]])
                    T.op("dve", lambda h=h, si=si: V.tensor_tensor(ya[:, h, :], sg[si][:], oN[:, h, :], ALU.mult),
                         reads=[b_sg[si], b_oN], writes=[b_ya[h]])
            for g in range(D_MODEL // WC):
                wi = g % 2
                c0 = g * WC
                _wload(c, w16[wi], b_w16[wi], c.wb_in[l, :, 8272 + c0:8272 + c0 + WC], WC)
                _wload(c, w16[2 + wi], b_w16[2 + wi], c.wb_in[l, :, 10320 + c0:10320 + c0 + WC], WC)
                _wload(c, w8[wi], b_w8[wi], c.wb_ba[l, :, c0:c0 + WC], WC)
                _wload(c, w8[2 + wi], b_w8[2 + wi], c.wb_bb[l, :, c0:c0 + WC], WC)
                for j in range(WC // 128):
                    dc = g * (WC // 128) + j
                    js = slice(j * 128, (j + 1) * 128)
                    T.op("pe", lambda wi=wi, js=js: _grp([(lambda kc=kc: PE.matmul(
                        ps[0][:], w16[wi][:, kc, js], xT[:, kc, :], start=(kc == 0), stop=(kc == KC - 1))) for kc in range(KC)]),
                         reads=[b_w16[wi], b_xT], writes=[b_ps[0]])
                    T.op("pe", lambda wi=wi, js=js: _grp([(lambda kc=kc: PE.matmul(
                        ps[1][:], w8[wi][:, kc, js], ya[:, kc, :], start=(kc == 0), stop=(kc == 7))) for kc in range(8)]),
                         reads=[b_w8[wi]] + b_ya, writes=[b_ps[1]])
                    T.op("pe", lambda wi=wi, js=js: _grp([(lambda kc=kc: PE.matmul(
                        ps[2][:], w16[2 + wi][:, kc, js], xT[:, kc, :], start=(kc == 0), stop=(kc == KC - 1))) for kc in range(KC)]),
                         reads=[b_w16[2 + wi], b_xT], writes=[b_ps[2]])
                    T.op("pe", lambda wi=wi, js=js: _grp([(lambda kc=kc: PE.matmul(
                        ps[3][:], w8[2 + wi][:, kc, js], yb[:, kc, :], start=(kc == 0), stop=(kc == 7))) for kc in range(8)]),
                         reads=[b_w8[2 + wi], b_yb], writes=[b_ps[3]])
                    T.op("act", lambda: A.activation(sg[0][:], ps[0][:], AF.Sigmoid), reads=[b_ps[0]], writes=[b_sg[0]])
                    T.op("act", lambda: A.activation(sg[1][:], ps[2][:], AF.Sigmoid), reads=[b_ps[2]], writes=[b_sg[1]])
                    T.op("dve", lambda: V.tensor_tensor(tm[0][:], sg[0][:], ps[1][:], ALU.mult), reads=[b_sg[0], b_ps[1]], writes=[b_tm[0]])
                    T.op("dve", lambda: V.tensor_tensor(tm[1][:], sg[1][:], ps[3][:], ALU.mult), reads=[b_sg[1], b_ps[3]], writes=[b_tm[1]])
                    T.op("pool", lambda dc=dc: G.tensor_tensor(mg[:, dc, :], tm[0][:], tm[1][:], ALU.add), reads=b_tm, writes=[b_mg[dc]])
            for g in range(D_MODEL // WC):
                wi = g % 2
                c0 = g * WC
                _wload(c, w16[wi], b_w16[wi], c.wb_out[l, :, c0:c0 + WC], WC)
                for j in range(WC // 128):
                    dc = g * (WC // 128) + j
                    js = slice(j * 128, (j + 1) * 128)
                    pi = dc % 4
                    ri = dc % 2
                    T.dma("sp", xr[ri][:], c.xres_d[dc, :, tsl], writes=[b_xr[ri]])
                    T.op("pe", lambda wi=wi, js=js, pi=pi: _grp([(lambda kc=kc: PE.matmul(
                        ps[pi][:], w16[wi][:, kc, js], mg[:, kc, :], start=(kc == 0), stop=(kc == KC - 1))) for kc in range(KC)]),
                         reads=[b_w16[wi]] + b_mg, writes=[b_ps[pi]])
                    T.op("dve", lambda dc=dc, ri=ri, pi=pi: V.scalar_tensor_tensor(z[:, dc, :], xr[ri][:], ALPHA, ps[pi][:],
                                                                                   ALU.mult, ALU.add),
                         reads=[b_xr[ri], b_ps[pi]], writes=[b_z[dc]])
                    _ln_stats(c, z, b_z, dc, zb, b_zb)
            _ln_finish(c, z, b_z, l, 0, xb, b_xb, scratch, b_scr)
            T.dma("pool", c.x1res_d[:, :, tsl].rearrange("k p t -> p k t"), z[:], reads=b_z)
            T.dma("pool", c.x1T_d[:, :, tsl].rearrange("k p t -> p k t"), xb[:], reads=b_xb)
        T.barrier()


def phase_stageB(c, l, last):
    nc, T = c.nc, c.T
    PE, V, A, G = nc.tensor, nc.vector, nc.scalar, nc.gpsimd
    ps, b_ps = c.ps, c.b_ps
    WC = 256
    NH = D_FF // 128
    with ExitStack() as es:
        mk = _tiles(es, nc)
        xb = mk("xb", [128, KC, TT], BF16)
        z = mk("z", [128, KC, TT], F32)
        hT = mk("hT", [128, NH, TT], BF16)
        pT = mk("pT", [128, 2, TT], BF16)
        wbuf = [mk("wbuf%d" % i, [128, NH, WC], BF16) for i in range(2)]
        wbuf2 = [mk("wbuf2_%d" % i, [128, KC, WC], BF16) for i in range(2)]
        sg = [mk("sg%d" % i, [128, TT], F32) for i in range(2)]
        tm = [mk("tm%d" % i, [128, TT], F32) for i in range(2)]
        scratch = [mk("lns%d" % i, [128, TT], F32) for i in range(3)]
        zb = [mk("zb%d" % i, [128, 2, TT], BF16) for i in range(2)]
        b_xbw = T.buf()
        b_xb = T.bufs(KC)
        b_z = T.bufs(KC)
        b_h = T.bufs(NH)
        b_pT = T.buf()
        b_w, b_w2, b_sg, b_tm, b_scr, b_zb = T.bufs(2), T.bufs(2), T.bufs(2), T.bufs(2), T.bufs(3), T.bufs(2)
        for tt in range(SEQ // TT):
            tsl = slice(tt * TT, (tt + 1) * TT)
            T.dma("sp", xb[:], c.x1T_d[:, :, tsl].rearrange("k p t -> p k t"), writes=b_xb)
            T.dma("sp", z[:], c.x1res_d[:, :, tsl].rearrange("k p t -> p k t"), writes=b_z)
            T.dma("pool", pT[:], c.pT_in[l, :, :, tsl].rearrange("k p t -> p k t"), writes=[b_pT])
            for g in range(D_FF // WC):
                wi = g % 2
                c0 = g * WC
                _wload(c, wbuf[wi][:, 0:KC, :], b_w[wi], c.wb_fg[l, :, c0:c0 + WC], WC)
                _wload(c, wbuf2[wi], b_w2[wi], c.wb_fu[l, :, c0:c0 + WC], WC)
                for j in range(WC // 128):
                    hc = g * (WC // 128) + j
                    js = slice(j * 128, (j + 1) * 128)
                    p0, p1 = (0, 1) if hc % 2 == 0 else (2, 3)
                    si = hc % 2
                    T.op("pe", lambda wi=wi, js=js, p0=p0: _grp([(lambda kc=kc: PE.matmul(
                        ps[p0][:], wbuf[wi][:, kc, js], xb[:, kc, :], start=(kc == 0), stop=(kc == KC - 1))) for kc in range(KC)]),
                         reads=[b_w[wi]] + b_xb, writes=[b_ps[p0]])
                    T.op("pe", lambda wi=wi, js=js, p1=p1: _grp([(lambda kc=kc: PE.matmul(
                        ps[p1][:], wbuf2[wi][:, kc, js], xb[:, kc, :], start=(kc == 0), stop=(kc == KC - 1))) for kc in range(KC)]),
                         reads=[b_w2[wi]] + b_xb, writes=[b_ps[p1]])
                    T.op("act", lambda si=si, p0=p0: A.activation(sg[si][:], ps[p0][:], AF.Silu), reads=[b_ps[p0]], writes=[b_sg[si]])
                    T.op("dve", lambda hc=hc, si=si, p1=p1: V.tensor_tensor(hT[:, hc, :], sg[si][:], ps[p1][:], ALU.mult),
                         reads=[b_sg[si], b_ps[p1]], writes=[b_h[hc]])
            for g in range(D_MODEL // WC):
                wi = g % 2
                c0 = g * WC
                _wload(c, wbuf[wi], b_w[wi], c.wb_fd[l, :, c0:c0 + WC], WC)
                for j in range(WC // 128):
                    dc = g * (WC // 128) + j
                    js = slice(j * 128, (j + 1) * 128)
                    pi = dc % 4
                    T.op("pe", lambda wi=wi, js=js, pi=pi: _grp([(lambda kc=kc: PE.matmul(
                        ps[pi][:], wbuf[wi][:, kc, js], hT[:, kc, :], start=(kc == 0), stop=(kc == NH - 1))) for kc in range(NH)]),
                         reads=[b_w[wi]] + b_h, writes=[b_ps[pi]])
                    T.op("dve", lambda dc=dc, pi=pi: V.scalar_tensor_tensor(z[:, dc, :], z[:, dc, :], ALPHA, ps[pi][:], ALU.mult, ALU.add),
                         reads=[b_z[dc], b_ps[pi]], writes=[b_z[dc]])
                    _ln_stats(c, z, b_z, dc, zb, b_zb)
            _ln_finish(c, z, b_z, l, 2, xb, b_xb, scratch, b_scr)
            for g in range(D_MODEL // WC):
                wi = g % 2
                c0 = g * WC
                _wload(c, wbuf2[wi], b_w2[wi], c.wb_pg[l, :, c0:c0 + WC], WC)
                _wload(c, wbuf[wi][:, 0:2, :], b_w[wi], c.wb_pp[l, :, c0:c0 + WC], WC)
                for j in range(WC // 128):
                    dc = g * (WC // 128) + j
                    js = slice(j * 128, (j + 1) * 128)
                    p0, p1 = (0, 1) if dc % 2 == 0 else (2, 3)
                    si = dc % 2
                    T.op("pe", lambda wi=wi, js=js, p0=p0: _grp([(lambda kc=kc: PE.matmul(
                        ps[p0][:], wbuf2[wi][:, kc, js], xb[:, kc, :], start=(kc == 0), stop=(kc == KC - 1))) for kc in range(KC)]),
                         reads=[b_w2[wi]] + b_xb, writes=[b_ps[p0]])
                    T.op("pe", lambda wi=wi, js=js, p1=p1: _grp([(lambda kc=kc: PE.matmul(
                        ps[p1][:], wbuf[wi][:, kc, js], pT[:, kc, :], start=(kc == 0), stop=(kc == 1))) for kc in range(2)]),
                         reads=[b_w[wi], b_pT], writes=[b_ps[p1]])
                    T.op("act", lambda si=si, p0=p0: A.activation(sg[si][:], ps[p0][:], AF.Sigmoid), reads=[b_ps[p0]], writes=[b_sg[si]])
                    T.op("dve", lambda si=si, p1=p1: V.tensor_tensor(tm[si][:], sg[si][:], ps[p1][:], ALU.mult),
                         reads=[b_sg[si], b_ps[p1]], writes=[b_tm[si]])
                    T.op("dve", lambda dc=dc, si=si: V.scalar_tensor_tensor(z[:, dc, :], z[:, dc, :], ALPHA, tm[si][:], ALU.mult, ALU.add),
                         reads=[b_z[dc], b_tm[si]], writes=[b_z[dc]])
                    _ln_stats(c, z, b_z, dc, zb, b_zb)
            _ln_finish(c, z, b_z, l, 4, xb, b_xb, scratch, b_scr)
            if last:
                T.dma("pool", c.out_d[:, :, tsl].rearrange("k p t -> p k t"), z[:], reads=b_z)
            else:
                T.dma("pool", c.xres_d[:, :, tsl].rearrange("k p t -> p k t"), z[:], reads=b_z)
                T.dma("pool", c.xT_d[:, :, tsl].rearrange("k p t -> p k t"), xb[:], reads=b_xb)
        T.barrier()


def build_full(n_layers=DEPTH, debug=False):
    c = build_program(n_layers, debug)
    c.n_layers = n_layers
    phase_init(c)
    for l in range(n_layers):
        phase_proj(c, l)
        with ExitStack() as shared_es:
            c.shared_es = shared_es
            gens = [phase_hgrn(c, l), phase_dsa_prep(c, l)]
            while gens:
                for g in list(gens):
                    try:
                        next(g)
                    except StopIteration:
                        gens.remove(g)
            c.T.barrier()
        phase_dsa(c, l)
        phase_stageA(c, l)
        phase_stageB(c, l, last=(l == n_layers - 1))
    c.T.barrier()
    return c


def _consts():
    cs = np.zeros((128, 6, 128), np.float32)
    cs[:, 0, :] = np.eye(128, dtype=np.float32)
    s = np.arange(128)[:, None]
    t = np.arange(128)[None, :]
    cs[:, 1, :] = ((s // 64 == t // 64) & (s <= t)).astype(np.float32)
    cs[:, 2, :] = 1.0
    cs[:64, 3, 0] = 1.0
    cs[64:, 3, 1] = 1.0
    invf = np.zeros((128, 96), np.float32)
    invf[:, 0:64] = (10000.0 ** (-np.arange(64, dtype=np.float32) / np.float32(64))).astype(np.float32)[None, :]
    invf[:, 64:96] = (10000.0 ** (-np.arange(32, dtype=np.float32) / np.float32(32))).astype(np.float32)[None, :]
    return cs, invf


def make_in_maps(inputs, n_cores=8):
    f = lambda a: np.ascontiguousarray(np.asarray(a, dtype=np.float32))
    cs, invf = _consts()
    shared = {k: f(inputs[k]) for k in ("w_in", "w_branch_a", "w_branch_b", "w_out", "w_ffn_gate", "w_ffn_up",
                                          "w_ffn_down", "w_ple_gate", "w_ple_proj")}
    shared["consts"] = cs
    shared["invf"] = invf
    shared["hlb_rep"] = np.ascontiguousarray(np.broadcast_to(f(inputs["hgrn_lower_bounds"])[None], (128, DEPTH, A_W)))
    kn = np.stack([f(inputs["idx_k_norm_g"]), f(inputs["idx_k_norm_b"])], axis=1)
    shared["kn_rep"] = np.ascontiguousarray(np.broadcast_to(kn[None], (128, DEPTH, 2, IDX_DIM)))
    shared["hng_t"] = np.ascontiguousarray(f(inputs["hgrn_norm_g"]).T)
    lnp = np.stack([f(inputs[k]) for k in ("ln_mix_g", "ln_mix_b", "ln_ffn_g", "ln_ffn_b", "ln_ple_g", "ln_ple_b")], axis=1)
    shared["lnp_t"] = np.ascontiguousarray(lnp.reshape(DEPTH, 6, KC, 128).transpose(3, 0, 1, 2))
    x = f(inputs["x"])
    p = f(inputs["p"])
    pos = np.asarray(inputs["positions"]).astype(np.int32)
    maps = []
    for core in range(n_cores):
        b = core // 2
        m = dict(shared)
        m["xT_in"] = np.ascontiguousarray(x[b].T.reshape(KC, 128, SEQ))
        m["pT_in"] = np.ascontiguousarray(p[:, b].transpose(0, 2, 1).reshape(DEPTH, 2, 128, SEQ))
        m["pos_in"] = np.ascontiguousarray(pos[b].reshape(NBLK, 128).T)
        maps.append(m)
    return maps


def kernel(**inputs):
    c = build_full(DEPTH)
    maps = make_in_maps(inputs)
    res = run_bass_kernel_spmd(c.nc, maps, core_ids=list(range(8)))
    out = np.empty((4, SEQ, D_MODEL), np.float32)
    for b in range(4):
        o0 = res.results[2 * b]["outT"].reshape(D_MODEL, SEQ)
        o1 = res.results[2 * b + 1]["outT"].reshape(D_MODEL, SEQ)
        out[b, : SEQ // 2] = o0[:, : SEQ // 2].T
        out[b, SEQ // 2:] = o1[:, SEQ // 2:].T
    return out
```

```python
import numpy as np
import concourse.bass as bass
import concourse.mybir as mybir
from concourse.bass_utils import run_bass_kernel_spmd

F32 = mybir.dt.float32
BF16 = mybir.dt.bfloat16
I32 = mybir.dt.int32
AF = mybir.ActivationFunctionType
ALU = mybir.AluOpType
AX = mybir.AxisListType

D_MODEL = 2048
SEQ = 4096
DEPTH = 4
CHUNK = 64
PLE_DIM = 256
A_W = 1024
B_W = 1024
IDX_HEADS = 16
IDX_DIM = 64
TOPK = 256
D_FF = 5632
ALPHA = (2 * DEPTH) ** 0.25
LN_EPS = 1e-5
RMS_EPS = 1e-6
ATTN_SCALE = 128 ** -0.5
IDX_SCALE = (IDX_HEADS * IDX_DIM) ** -0.5
IN_W = 12368
KC = D_MODEL // 128
NBLK = SEQ // 128
TT = 512
NEG = -1.0e30


class Buf:
    __slots__ = ("name", "writer", "readers")

    def __init__(self, name):
        self.name = name
        self.writer = None
        self.readers = []


class Trk:
    def __init__(self, nc, n_dma_sems=40):
        self.nc = nc
        self.eng = {"pe": nc.tensor, "act": nc.scalar, "dve": nc.vector, "pool": nc.gpsimd, "sp": nc.sync}
        self.sems = {}
        self.count = {}
        for e in ("pe", "act", "dve", "pool"):
            self.sems[e] = nc.alloc_semaphore("s_" + e)
            self.count[e] = 0
        self.dma_keys = []
        for i in range(n_dma_sems):
            k = "d%d" % i
            self.sems[k] = nc.alloc_semaphore("s_" + k)
            self.count[k] = 0
            self.dma_keys.append(k)
        self.dma_rr = 0
        self.qkeys = {"sp": self.dma_keys[:28], "pool": self.dma_keys[28:]}
        self.qrr = {"sp": 0, "pool": 0}
        self.waited = {e: {} for e in self.eng}
        self.nbuf = 0
        self.ninst = 0

    def buf(self, name=None):
        self.nbuf += 1
        return Buf(name or "b%d" % self.nbuf)

    def bufs(self, n, name="b"):
        return [self.buf("%s%d" % (name, i)) for i in range(n)]

    def _wait(self, e, tok):
        if tok is None:
            return
        k, v = tok
        if e == "pe" and k == "pe":
            return
        w = self.waited[e]
        if w.get(k, 0) >= v:
            return
        w[k] = v
        self.eng[e].wait_ge(self.sems[k], v)

    def _deps(self, e, reads, writes):
        for b in reads:
            self._wait(e, b.writer)
        for b in writes:
            self._wait(e, b.writer)
            for r in b.readers:
                self._wait(e, r)

    def _commit(self, tok, reads, writes):
        for b in reads:
            b.readers.append(tok)
        for b in writes:
            b.writer = tok
            b.readers = []

    def op(self, e, fn, reads=(), writes=()):
        self._deps(e, reads, writes)
        ins = fn()
        self.count[e] += 1
        tok = (e, self.count[e])
        ins.then_inc(self.sems[e], 1)
        self._commit(tok, reads, writes)
        self.ninst += 1
        return tok

    def dma(self, q, out, in_, reads=(), writes=(), **kw):
        self._deps(q, reads, writes)
        keys = self.qkeys[q]
        k = keys[self.qrr[q]]
        self.qrr[q] = (self.qrr[q] + 1) % len(keys)
        if self.count[k] > 0:
            self._wait(q, (k, self.count[k]))
        self.count[k] += 16
        tok = (k, self.count[k])
        self.eng[q].dma_start(out=out, in_=in_, **kw).then_inc(self.sems[k], 16)
        self._commit(tok, reads, writes)
        self.ninst += 1
        return tok

    def barrier(self):
        for e in self.eng:
            for k in self.sems:
                if self.count[k] > 0:
                    self._wait(e, (k, self.count[k]))


class Ctx:
    pass


def _grp(fns):
    ins = None
    for f in fns:
        ins = f()
    return ins


def build_program(n_layers=DEPTH, debug=False):
    nc = bass.Bass("TRN2", target_bir_lowering=False)
    T = Trk(nc)
    c = Ctx()
    c.nc, c.T = nc, T
    S = SEQ

    def din(name, shape, dt=F32):
        return nc.dram_tensor(name, list(shape), dt, kind="ExternalInput").ap()

    def dint(name, shape, dt):
        kind = "ExternalOutput" if debug else "Internal"
        return nc.dram_tensor(name, list(shape), dt, kind=kind).ap()

    def dscr(name, shape, dt):
        return nc.dram_tensor(name, list(shape), dt, kind="Internal").ap()

    xT_in = din("xT_in", [KC, 128, S])
    pT_in = din("pT_in", [DEPTH, 2, 128, S])
    pos_in = din("pos_in", [128, NBLK], I32)
    consts = din("consts", [128, 6, 128])
    invf = din("invf", [128, 96])
    w_in = din("w_in", [DEPTH, D_MODEL, IN_W])
    w_ba = din("w_branch_a", [DEPTH, A_W, D_MODEL])
    w_bb = din("w_branch_b", [DEPTH, B_W, D_MODEL])
    w_out = din("w_out", [DEPTH, D_MODEL, D_MODEL])
    w_fg = din("w_ffn_gate", [DEPTH, D_MODEL, D_FF])
    w_fu = din("w_ffn_up", [DEPTH, D_MODEL, D_FF])
    w_fd = din("w_ffn_down", [DEPTH, D_FF, D_MODEL])
    w_pg = din("w_ple_gate", [DEPTH, D_MODEL, D_MODEL])
    w_pp = din("w_ple_proj", [DEPTH, PLE_DIM, D_MODEL])
    hlb_in = din("hlb_rep", [128, DEPTH, A_W])
    kng_in = din("kn_rep", [128, DEPTH, 2, IDX_DIM])
    hng_in = din("hng_t", [128, DEPTH])
    lnp_in = din("lnp_t", [128, DEPTH, 6, KC])
    out_d = nc.dram_tensor("outT", [KC, 128, S], F32, kind="ExternalOutput").ap()

    wb_in = dscr("wb_in", [DEPTH, D_MODEL, IN_W], BF16)
    wb_ba = dscr("wb_ba", [DEPTH, A_W, D_MODEL], BF16)
    wb_bb = dscr("wb_bb", [DEPTH, B_W, D_MODEL], BF16)
    wb_out = dscr("wb_out", [DEPTH, D_MODEL, D_MODEL], BF16)
    wb_fg = dscr("wb_fg", [DEPTH, D_MODEL, D_FF], BF16)
    wb_fu = dscr("wb_fu", [DEPTH, D_MODEL, D_FF], BF16)
    wb_fd = dscr("wb_fd", [DEPTH, D_FF, D_MODEL], BF16)
    wb_pg = dscr("wb_pg", [DEPTH, D_MODEL, D_MODEL], BF16)
    wb_pp = dscr("wb_pp", [DEPTH, PLE_DIM, D_MODEL], BF16)
    xT_d = dscr("xT_d", [KC, 128, S], BF16)
    xres_d = dscr("xres_d", [KC, 128, S], F32)
    x1T_d = dscr("x1T_d", [KC, 128, S], BF16)
    x1res_d = dscr("x1res_d", [KC, 128, S], F32)
    NPROJ = 7248
    projT = dscr("projT", [S, NPROJ], F32)
    lb_d = dscr("lb_d", [DEPTH, 128, A_W], F32)
    oN_d = dint("oN_d", [8, 128, S], BF16)
    qT_d = dscr("qT_d", [8, 128, S], BF16)
    kT_d = dscr("kT_d", [8, 128, S], BF16)
    v_d = dscr("v_d", [S, B_W], BF16)
    iqT_d = dscr("iqT_d", [8, 128, S], BF16)
    ikT_d = dscr("ikT_d", [128, S], BF16)
    iw_d = dscr("iw_d", [S, IDX_HEADS], F32)
    ybT_d = dint("ybT_d", [8, 128, S], BF16)

    sb = nc.alloc_sbuf_tensor
    cst = sb("cst", [128, 6, 128], F32)
    identb = sb("identb", [128, 128], BF16)
    tri2b = sb("tri2b", [128, 128], BF16)
    onesb = sb("onesb", [128, 128], BF16)
    hng = sb("hng", [128, DEPTH], F32)
    lnp = sb("lnp", [128, DEPTH, 6, KC], F32)
    cosH = sb("cosH", [128, NBLK, 64], F32)
    sinH = sb("sinH", [128, NBLK, 64], F32)
    cosI = sb("cosI", [128, NBLK, 32], F32)
    sinI = sb("sinI", [128, NBLK, 32], F32)
    b_cst = T.buf("cst")
    b_rope = T.buf("rope")
    tri2f = cst[:, 1, :]
    cindf = cst[:, 3, 0:2]
    ps = [nc.alloc_psum_tensor("ps%d" % i, [128, 512], F32) for i in range(8)]
    psb = [ps[6][:].bitcast(BF16), ps[7][:].bitcast(BF16)]
    b_ps = T.bufs(8, "ps")
    b_psb = [b_ps[6], b_ps[7]]

    c.__dict__.update(locals())
    return c


from contextlib import ExitStack

_uid = [0]


def _tiles(es, nc):
    def mk(name, shape, dt):
        _uid[0] += 1
        return es.enter_context(nc.sbuf_tensor("%s_%d" % (name, _uid[0]), list(shape), dt))
    return mk


TWO_PI = float(2 * np.pi)
CW1 = 6.28125
CW2 = 0.0019353071795864769


def conv_list(c, l):
    T = c.T
    out = []
    for (dst, src, rows) in ((c.wb_in, c.w_in, D_MODEL), (c.wb_ba, c.w_ba, A_W), (c.wb_bb, c.w_bb, B_W),
                             (c.wb_out, c.w_out, D_MODEL), (c.wb_fg, c.w_fg, D_MODEL), (c.wb_fu, c.w_fu, D_MODEL),
                             (c.wb_fd, c.w_fd, D_FF), (c.wb_pg, c.w_pg, D_MODEL), (c.wb_pp, c.w_pp, PLE_DIM)):
        for r0 in range(0, rows, 128):
            out.append(lambda dst=dst, src=src, r0=r0: T.dma("pool", dst[l, r0:r0 + 128, :], src[l, r0:r0 + 128, :]))
    return out


def phase_init(c):
    nc, T = c.nc, c.T
    V, A = nc.vector, nc.scalar
    T.dma("sp", c.cst[:], c.consts[:, :, :], writes=[c.b_cst])
    T.dma("sp", c.hng[:], c.hng_in[:, :], writes=[c.b_cst])
    T.dma("sp", c.lnp[:], c.lnp_in[:, :, :, :], writes=[c.b_cst])
    T.op("dve", lambda: V.tensor_copy(c.identb[:], c.cst[:, 0, :]), reads=[c.b_cst], writes=[c.b_cst])
    T.op("dve", lambda: V.tensor_copy(c.tri2b[:], c.cst[:, 1, :]), reads=[c.b_cst], writes=[c.b_cst])
    T.op("dve", lambda: V.tensor_copy(c.onesb[:], c.cst[:, 2, :]), reads=[c.b_cst], writes=[c.b_cst])
    for kc in range(KC):
        T.dma("pool", c.xT_d[kc], c.xT_in[kc])
        T.dma("sp", c.xres_d[kc], c.xT_in[kc])
    for l in range(c.n_layers):
        for fn in conv_list(c, l):
            fn()

    with ExitStack() as es:
        mk = _tiles(es, nc)
        posi = mk("posi", [128, NBLK], I32)
        posf = mk("posf", [128, NBLK], F32)
        ivf = mk("ivf", [128, 96], F32)
        ang = mk("ang", [128, NBLK * 64], F32)
        kf = mk("kf", [128, NBLK * 64], F32)
        ki = mk("ki", [128, NBLK * 64], I32)
        rr = mk("rr", [128, NBLK * 64], F32)
        b_pos, b_ang, b_kf, b_ki, b_rr = T.bufs(5, "rp")
        T.dma("sp", posi[:], c.pos_in[:, :], writes=[b_pos])
        T.dma("sp", ivf[:], c.invf[:, :], writes=[b_pos])
        T.op("dve", lambda: V.tensor_copy(posf[:], posi[:]), reads=[b_pos], writes=[b_pos])
        for (half, off, cosT, sinT) in ((64, 0, c.cosH, c.sinH), (32, 64, c.cosI, c.sinI)):
            n = NBLK * half
            for b in range(NBLK):
                T.op("dve", lambda b=b: V.tensor_scalar(ang[:, b * half:(b + 1) * half], ivf[:, off:off + half],
                                                        posf[:, b:b + 1], None, ALU.mult),
                     reads=[b_pos], writes=[b_ang])
            for (shift, dst) in ((0.0, sinT), (0.25, cosT)):
                T.op("dve", lambda: V.tensor_scalar(kf[:, 0:n], ang[:, 0:n], 1.0 / TWO_PI, shift, ALU.mult, ALU.add),
                     reads=[b_ang], writes=[b_kf])
                T.op("dve", lambda: V.tensor_copy(ki[:, 0:n], kf[:, 0:n]), reads=[b_kf], writes=[b_ki])
                T.op("dve", lambda: V.tensor_copy(kf[:, 0:n], ki[:, 0:n]), reads=[b_ki], writes=[b_kf])
                T.op("dve", lambda: V.scalar_tensor_tensor(rr[:, 0:n], kf[:, 0:n], -CW1, ang[:, 0:n], ALU.mult, ALU.add),
                     reads=[b_kf, b_ang], writes=[b_rr])
                T.op("dve", lambda: V.scalar_tensor_tensor(rr[:, 0:n], kf[:, 0:n], -CW2, rr[:, 0:n], ALU.mult, ALU.add),
                     reads=[b_kf, b_rr], writes=[b_rr])
                if shift:
                    T.op("dve", lambda: V.tensor_scalar(rr[:, 0:n], rr[:, 0:n], float(np.pi / 2), None, ALU.add),
                         reads=[b_rr], writes=[b_rr])
                T.op("dve", lambda: V.tensor_scalar(kf[:, 0:n], rr[:, 0:n], float(np.pi), None, ALU.is_gt),
                     reads=[b_rr], writes=[b_kf])
                T.op("dve", lambda: V.scalar_tensor_tensor(rr[:, 0:n], kf[:, 0:n], -TWO_PI, rr[:, 0:n], ALU.mult, ALU.add),
                     reads=[b_kf, b_rr], writes=[b_rr])
                T.op("dve", lambda: V.tensor_scalar(kf[:, 0:n], rr[:, 0:n], float(-np.pi), None, ALU.is_lt),
                     reads=[b_rr], writes=[b_kf])
                T.op("dve", lambda: V.scalar_tensor_tensor(rr[:, 0:n], kf[:, 0:n], TWO_PI, rr[:, 0:n], ALU.mult, ALU.add),
                     reads=[b_kf, b_rr], writes=[b_rr])
                T.op("dve", lambda: V.tensor_scalar(rr[:, 0:n], rr[:, 0:n], float(np.pi), float(-np.pi), ALU.min, ALU.max),
                     reads=[b_rr], writes=[b_rr])
                T.op("act", lambda dst=dst: A.activation(dst[:].rearrange("p b j -> p (b j)"), rr[:, 0:n], AF.Sin),
                     reads=[b_rr], writes=[c.b_rope])
        hl = mk("hl", [128, DEPTH, A_W], F32)
        ssum = mk("ssum", [128, A_W], F32)
        acc = mk("acc", [128, A_W], F32)
        lbt = mk("lbt", [128, A_W], F32)
        b_hl, b_ss, b_acc, b_lbt = T.bufs(4, "lb")
        T.dma("sp", hl[:], c.hlb_in[:, :, :], writes=[b_hl])
        T.op("act", lambda: A.activation(hl[:].rearrange("p l n -> p (l n)"), hl[:].rearrange("p l n -> p (l n)"), AF.Exp),
             reads=[b_hl], writes=[b_hl])
        T.op("dve", lambda: V.tensor_tensor(ssum[:], hl[:, 0, :], hl[:, 1, :], ALU.add), reads=[b_hl], writes=[b_ss])
        T.op("dve", lambda: V.tensor_tensor(ssum[:], ssum[:], hl[:, 2, :], ALU.add), reads=[b_hl, b_ss], writes=[b_ss])
        T.op("dve", lambda: V.tensor_tensor(ssum[:], ssum[:], hl[:, 3, :], ALU.add), reads=[b_hl, b_ss], writes=[b_ss])
        T.op("dve", lambda: V.reciprocal(ssum[:], ssum[:]), reads=[b_ss], writes=[b_ss])
        T.op("dve", lambda: V.memset(acc[:], 0.0), writes=[b_acc])
        for l in range(DEPTH):
            if l > 0:
                T.op("dve", lambda l=l: V.tensor_tensor(acc[:], acc[:], hl[:, l, :], ALU.add),
                     reads=[b_hl, b_acc], writes=[b_acc])
            T.op("dve", lambda: V.tensor_tensor(lbt[:], acc[:], ssum[:], ALU.mult), reads=[b_acc, b_ss], writes=[b_lbt])
            T.dma("sp", c.lb_d[l], lbt[:], reads=[b_lbt])
        T.barrier()


def phase_proj(c, l):
    nc, T = c.nc, c.T
    PE = nc.tensor
    groups = []
    for (w0, w1, p0) in ((0, 3072, 0), (4096, 8272, 3072)):
        cc = w0
        while cc < w1:
            n = min(512, w1 - cc)
            groups.append((cc, n, p0 + (cc - w0)))
            cc += n
    with ExitStack() as es:
        mk = _tiles(es, nc)
        xh = mk("xh", [128, KC, 2048], BF16)
        wg = [mk("wg%d" % i, [128, KC, 512], BF16) for i in range(2)]
        st = [mk("st%d" % i, [128, 512], F32) for i in range(4)]
        b_xh = T.buf("xh")
        b_wg = T.bufs(2, "wg")
        b_st = T.bufs(4, "st")
        cnt = 0
        for half in range(2):
            T.dma("sp", xh[:], c.xT_d[:, :, half * 2048:(half + 1) * 2048].rearrange("k p t -> p k t"), writes=[b_xh])
            for gi, (w0, n, p0) in enumerate(groups):
                wi = gi % 2
                T.dma("sp", wg[wi][:, :, 0:n],
                      c.wb_in[l, :, w0:w0 + n].rearrange("(k p) n -> p k n", p=128), writes=[b_wg[wi]])
                for tb in range(16):
                    pi = cnt % 4
                    si = cnt % 4
                    cnt += 1
                    T.op("pe", lambda tb=tb, pi=pi, wi=wi, n=n: _grp([
                        (lambda kc=kc: PE.matmul(c.ps[pi][:, 0:n], xh[:, kc, tb * 128:(tb + 1) * 128], wg[wi][:, kc, 0:n],
                                                 start=(kc == 0), stop=(kc == KC - 1))) for kc in range(KC)]),
                         reads=[b_xh, b_wg[wi]], writes=[c.b_ps[pi]])
                    if cnt % 2:
                        T.op("act", lambda pi=pi, si=si, n=n: nc.scalar.copy(st[si][:, 0:n], c.ps[pi][:, 0:n]),
                             reads=[c.b_ps[pi]], writes=[b_st[si]])
                    else:
                        T.op("dve", lambda pi=pi, si=si, n=n: nc.vector.tensor_copy(st[si][:, 0:n], c.ps[pi][:, 0:n]),
                             reads=[c.b_ps[pi]], writes=[b_st[si]])
                    t0 = (half * 16 + tb) * 128
                    T.dma("pool", c.projT[t0:t0 + 128, p0:p0 + n], st[si][:, 0:n], reads=[b_st[si]])
        T.barrier()


def phase_hgrn(c, l):
    nc, T = c.nc, c.T
    PE, V, A, G = nc.tensor, nc.vector, nc.scalar, nc.gpsimd
    ps, b_ps, psb, b_psb = c.ps, c.b_ps, c.psb, c.b_psb
    with ExitStack() as es:
        mk = _tiles(c.shared_es, nc)
        lbt = mk("lbt", [128, A_W], F32)
        oml = mk("oml", [128, A_W], F32)
        aq = [mk("aq%d" % i, [128, A_W], F32) for i in range(2)]
        af = [mk("af%d" % i, [128, A_W], F32) for i in range(2)]
        ai = [mk("ai%d" % i, [128, A_W], F32) for i in range(2)]
        ff = mk("ff", [128, A_W], F32)
        kk = mk("kk", [128, A_W], F32)
        logf = mk("logf", [128, A_W], F32)
        ecum = mk("ecum", [128, A_W], F32)
        encum = mk("encum", [128, A_W], F32)
        qf = mk("qf", [128, A_W], F32)
        qe = mk("qe", [128, A_W], BF16)
        ke = mk("ke", [128, A_W], BF16)
        vt = mk("vt", [128, A_W], BF16)
        qeT = mk("qeT", [128, 8, 128], BF16)
        keT = mk("keT", [128, 8, 128], BF16)
        atm = mk("atm", [128, 8, 128], BF16)
        St = mk("St", [128, 8, 128], F32)
        Sb = mk("Sb", [128, 8, 128], BF16)
        tmpS = [mk("tmpS%d" % i, [128, 128], F32) for i in range(2)]
        elast = mk("elast", [128, 16], F32)
        osq = mk("osq", [128, A_W], BF16)
        rstd = mk("rstd", [128, A_W], F32)
        oNt = [mk("oNt%d" % i, [128, A_W], BF16) for i in range(2)]
        b_lb = T.buf()
        b_aq, b_af, b_ai = T.bufs(2), T.bufs(2), T.bufs(2)
        b_ff, b_kk, b_logf, b_ecum, b_encum, b_qf, b_qe, b_ke, b_vt, b_qeT, b_keT, b_atm = T.bufs(12)
        b_S = T.bufs(8, "S")
        b_Sb = T.bufs(8, "Sb")
        b_tmpS = T.bufs(2)
        b_el, b_osq, b_rstd = T.bufs(3)
        b_oNt = T.bufs(2)
        b_pss = T.bufs(4, "pss")

        T.dma("sp", lbt[:], c.lb_d[l], writes=[b_lb])
        T.op("dve", lambda: V.tensor_scalar(oml[:], lbt[:], -1.0, 1.0, ALU.mult, ALU.add), reads=[b_lb], writes=[b_lb])
        T.op("dve", lambda: V.memset(St[:], 0.0), writes=b_S)
        T.op("dve", lambda: V.memset(Sb[:], 0.0), writes=b_Sb)

        def load(tb):
            i = tb % 2
            t0 = tb * 128
            T.dma("sp", aq[i][:], c.projT[t0:t0 + 128, 0:1024], writes=[b_aq[i]])
            T.dma("sp", af[i][:], c.projT[t0:t0 + 128, 1024:2048], writes=[b_af[i]])
            T.dma("sp", ai[i][:], c.projT[t0:t0 + 128, 2048:3072], writes=[b_ai[i]])

        load(0)
        for tb in range(NBLK):
            i = tb % 2
            if tb + 1 < NBLK:
                load(tb + 1)
            T.op("act", lambda: A.activation(ff[:], af[i][:], AF.Sigmoid), reads=[b_af[i]], writes=[b_ff])
            T.op("dve", lambda: V.tensor_tensor(ff[:], ff[:], oml[:], ALU.mult), reads=[b_ff, b_lb], writes=[b_ff])
            T.op("dve", lambda: V.tensor_tensor(ff[:], ff[:], lbt[:], ALU.add), reads=[b_ff, b_lb], writes=[b_ff])
            T.op("dve", lambda: V.tensor_scalar(kk[:], ff[:], -1.0, 1.0, ALU.mult, ALU.add), reads=[b_ff], writes=[b_kk])
            T.op("act", lambda: A.activation(logf[:], ff[:], AF.Ln), reads=[b_ff], writes=[b_logf])
            for j in range(2):
                T.op("pe", lambda j=j: PE.matmul(ps[j][:], c.tri2f, logf[:, j * 512:(j + 1) * 512], start=True, stop=True),
                     reads=[b_logf, c.b_cst], writes=[b_ps[j]])
            T.op("pe", lambda: _grp([(lambda h=h: PE.matmul(ps[2][:, 2 * h:2 * h + 2], logf[:, h * 128:(h + 1) * 128],
                                                            c.cindf, start=True, stop=True)) for h in range(8)]),
                 reads=[b_logf, c.b_cst], writes=[b_ps[2]])
            T.op("act", lambda: A.activation(elast[:], ps[2][:, 0:16], AF.Exp), reads=[b_ps[2]], writes=[b_el])
            for j in range(2):
                T.op("act", lambda j=j: A.activation(ecum[:, j * 512:(j + 1) * 512], ps[j][:], AF.Exp),
                     reads=[b_ps[j]], writes=[b_ecum])
                T.op("act", lambda j=j: A.activation(encum[:, j * 512:(j + 1) * 512], ps[j][:], AF.Exp, scale=-1.0),
                     reads=[b_ps[j]], writes=[b_encum])
            T.op("act", lambda: A.activation(qf[:], aq[i][:], AF.Silu), reads=[b_aq[i]], writes=[b_qf])
            T.op("dve", lambda: V.tensor_tensor(qe[:], qf[:], ecum[:], ALU.mult), reads=[b_qf, b_ecum], writes=[b_qe])
            T.op("dve", lambda: V.tensor_tensor(ke[:], kk[:], encum[:], ALU.mult), reads=[b_kk, b_encum], writes=[b_ke])
            T.op("pool", lambda: G.tensor_copy(vt[:], ai[i][:]), reads=[b_ai[i]], writes=[b_vt])
            T.op("pe", lambda: _grp([(lambda h=h: PE.transpose(psb[0][:, h * 128:(h + 1) * 128], qe[:, h * 128:(h + 1) * 128],
                                                               c.identb[:])) for h in range(8)]),
                 reads=[b_qe, c.b_cst], writes=[b_psb[0]])
            T.op("pe", lambda: _grp([(lambda h=h: PE.transpose(psb[1][:, h * 128:(h + 1) * 128], ke[:, h * 128:(h + 1) * 128],
                                                               c.identb[:])) for h in range(8)]),
                 reads=[b_ke, c.b_cst], writes=[b_psb[1]])
            T.op("act", lambda: A.copy(qeT[:].rearrange("p h t -> p (h t)"), psb[0][:]), reads=[b_psb[0]], writes=[b_qeT])
            T.op("dve", lambda: V.tensor_copy(keT[:].rearrange("p h t -> p (h t)"), psb[1][:]), reads=[b_psb[1]], writes=[b_keT])
            for g in range(2):
                T.op("pe", lambda g=g: _grp([(lambda h=h: PE.matmul(ps[3 + g][:, (h % 4) * 128:(h % 4 + 1) * 128],
                                                                    keT[:, h, :], qeT[:, h, :], start=True, stop=True))
                                             for h in range(4 * g, 4 * g + 4)]),
                     reads=[b_keT, b_qeT], writes=[b_ps[3 + g]])
                T.op("dve", lambda g=g: V.tensor_tensor(atm[:, 4 * g:4 * g + 4, :],
                                                        ps[3 + g][:].rearrange("p (h t) -> p h t", h=4),
                                                        c.tri2b[:].unsqueeze(1).to_broadcast([128, 4, 128]), ALU.mult),
                     reads=[b_ps[3 + g], c.b_cst], writes=[b_atm])
            for cch in range(2):
                for h in range(8):
                    pso = ps[h // 4]
                    oc = (h % 4) * 128
                    r0 = 64 * cch
                    T.op("pe", lambda h=h, r0=r0, pso=pso, oc=oc: _grp([
                        lambda: PE.matmul(pso[:, oc + r0:oc + r0 + 64], vt[r0:r0 + 64, h * 128:(h + 1) * 128],
                                          atm[r0:r0 + 64, h, r0:r0 + 64], start=True, stop=False),
                        lambda: PE.matmul(pso[:, oc + r0:oc + r0 + 64], Sb[:, h, :], qeT[:, h, r0:r0 + 64],
                                          start=False, stop=True)]),
                         reads=[b_vt, b_atm, b_Sb[h], b_qeT, b_ecum, b_encum], writes=[b_ps[h // 4]])
                    sl = h % 4
                    T.op("pe", lambda h=h, r0=r0, sl=sl: PE.matmul(ps[2 + sl][:, 0:128],
                                                                   ke[r0:r0 + 64, h * 128:(h + 1) * 128],
                                                                   vt[r0:r0 + 64, h * 128:(h + 1) * 128], start=True, stop=True),
                         reads=[b_ke, b_vt], writes=[b_ps[2 + sl]])
                    ti = h % 2
                    T.op("dve", lambda h=h, sl=sl, ti=ti: V.tensor_tensor(tmpS[ti][:], ps[2 + sl][:, 0:128],
                                                                          St[:, h, :], ALU.add),
                         reads=[b_ps[2 + sl], b_S[h]], writes=[b_tmpS[ti]])
                    T.op("dve", lambda h=h, ti=ti, cch=cch: V.tensor_scalar(St[:, h, :], tmpS[ti][:],
                                                                            elast[:, 2 * h + cch:2 * h + cch + 1], None, ALU.mult),
                         reads=[b_tmpS[ti], b_el], writes=[b_S[h]])
                    T.op("act", lambda h=h: A.copy(Sb[:, h, :], St[:, h, :]), reads=[b_S[h]], writes=[b_Sb[h]])
            for j in range(2):
                T.op("act", lambda j=j: A.activation(osq[:, j * 512:(j + 1) * 512], ps[j][:], AF.Square),
                     reads=[b_ps[j]], writes=[b_osq])
                T.op("pe", lambda j=j: PE.matmul(ps[3 + j][:], c.onesb[:], osq[:, j * 512:(j + 1) * 512], start=True, stop=True),
                     reads=[b_osq, c.b_cst], writes=[b_ps[3 + j]])
                T.op("act", lambda j=j: A.activation(rstd[:, j * 512:(j + 1) * 512], ps[3 + j][:], AF.Sqrt,
                                                     bias=RMS_EPS, scale=1.0 / 128.0),
                     reads=[b_ps[3 + j]], writes=[b_rstd])
            T.op("dve", lambda: V.reciprocal(rstd[:], rstd[:]), reads=[b_rstd], writes=[b_rstd])
            for j in range(2):
                T.op("dve", lambda j=j: V.scalar_tensor_tensor(oNt[i][:, j * 512:(j + 1) * 512], ps[j][:], c.hng[:, l:l + 1],
                                                               rstd[:, j * 512:(j + 1) * 512], ALU.mult, ALU.mult),
                     reads=[b_ps[j], b_rstd, c.b_cst], writes=[b_oNt[i]])
            T.dma("pool", c.oN_d[:, :, tb * 128:(tb + 1) * 128].rearrange("h p t -> p h t"),
                  oNt[i][:].rearrange("p (h t) -> p h t", h=8), reads=[b_oNt[i]])
            yield tb
        T.barrier()


def _rope(T, eng_name, eng, out_v, x_v, cos_b, sin_b, tmp, b_in, b_out, b_tmp, b_rope):
    x1, x2 = x_v[:, :, 0, :], x_v[:, :, 1, :]
    o1, o2 = out_v[:, :, 0, :], out_v[:, :, 1, :]
    t1, t2 = tmp
    T.op(eng_name, lambda: eng.tensor_tensor(t1, x1, cos_b, ALU.mult), reads=[b_in, b_rope], writes=[b_tmp[0]])
    T.op(eng_name, lambda: eng.tensor_tensor(t2, x2, sin_b, ALU.mult), reads=[b_in, b_rope], writes=[b_tmp[1]])
    T.op(eng_name, lambda: eng.tensor_tensor(o1, t1, t2, ALU.subtract), reads=[b_tmp[0], b_tmp[1]], writes=[b_out])
    T.op(eng_name, lambda: eng.tensor_tensor(t1, x1, sin_b, ALU.mult), reads=[b_in, b_rope], writes=[b_tmp[0]])
    T.op(eng_name, lambda: eng.tensor_tensor(t2, x2, cos_b, ALU.mult), reads=[b_in, b_rope], writes=[b_tmp[1]])
    T.op(eng_name, lambda: eng.tensor_tensor(o2, t1, t2, ALU.add), reads=[b_tmp[0], b_tmp[1]], writes=[b_out])


def phase_dsa_prep(c, l):
    nc, T = c.nc, c.T
    PE, V, A, G = nc.tensor, nc.vector, nc.scalar, nc.gpsimd
    ps, b_ps, psb, b_psb = c.ps, c.b_ps, c.psb, c.b_psb
    with ExitStack() as es:
        mk = _tiles(c.shared_es, nc)
        bq = [mk("bq%d" % i, [128, 1024], F32) for i in range(2)]
        bk = [mk("bk%d" % i, [128, 1024], F32) for i in range(2)]
        bv = [mk("bv%d" % i, [128, 1024], F32) for i in range(2)]
        iq = [mk("iq%d" % i, [128, 1024], F32) for i in range(2)]
        ikw = [mk("ikw%d" % i, [128, 80], F32) for i in range(2)]
        kn = mk("kn", [128, 2, 64], F32)
        tq = [mk("tq%d" % i, [128, 512], F32) for i in range(2)]
        tk = [mk("tk%d" % i, [128, 512], F32) for i in range(2)]
        qr = mk("qr", [128, 1024], BF16)
        kr = mk("kr", [128, 1024], BF16)
        iqr = mk("iqr", [128, 1024], BF16)
        vb = [mk("vb%d" % i, [128, 1024], BF16) for i in range(2)]
        qTt = [mk("qTt%d" % i, [128, 8, 128], BF16) for i in range(2)]
        kTt = [mk("kTt%d" % i, [128, 8, 128], BF16) for i in range(2)]
        iqTt = [mk("iqTt%d" % i, [128, 8, 128], BF16) for i in range(2)]
        ikT = [mk("ikT%d" % i, [128, 128], BF16) for i in range(2)]
        st1 = mk("st1", [128, 8], F32)
        xc = mk("xc", [128, 64], F32)
        sq = mk("sq", [128, 64], F32)
        t3 = [mk("t3%d" % i, [128, 32], F32) for i in range(2)]
        ikr = mk("ikr", [128, 128], BF16)
        iwt = [mk("iwt%d" % i, [128, 16], F32) for i in range(2)]
        b_bq, b_bk, b_bv, b_iq, b_ikw = T.bufs(2), T.bufs(2), T.bufs(2), T.bufs(2), T.bufs(2)
        b_kn, b_qr, b_kr, b_iqr, b_st1, b_xc, b_sq, b_ikr = T.bufs(8)
        b_tq, b_tk, b_t3 = T.bufs(2), T.bufs(2), T.bufs(2)
        b_vb, b_qTt, b_kTt, b_iqTt, b_ikT, b_iwt = T.bufs(2), T.bufs(2), T.bufs(2), T.bufs(2), T.bufs(2), T.bufs(2)
        T.dma("sp", kn[:], c.kng_in[:, l, :, :], writes=[b_kn])

        def load(tb):
            i = tb % 2
            t0 = tb * 128
            T.dma("sp", bq[i][:], c.projT[t0:t0 + 128, 3072:4096], writes=[b_bq[i]])
            T.dma("sp", bk[i][:], c.projT[t0:t0 + 128, 4096:5120], writes=[b_bk[i]])
            T.dma("sp", bv[i][:], c.projT[t0:t0 + 128, 5120:6144], writes=[b_bv[i]])
            T.dma("sp", iq[i][:], c.projT[t0:t0 + 128, 6144:7168], writes=[b_iq[i]])
            T.dma("sp", ikw[i][:], c.projT[t0:t0 + 128, 7168:7248], writes=[b_ikw[i]])

        load(0)
        for tb in range(NBLK):
            i = tb % 2
            tsl = slice(tb * 128, (tb + 1) * 128)
            if tb + 1 < NBLK:
                load(tb + 1)
            cosb = c.cosH[:, tb, :].unsqueeze(1).to_broadcast([128, 8, 64])
            sinb = c.sinH[:, tb, :].unsqueeze(1).to_broadcast([128, 8, 64])
            cosib = c.cosI[:, tb, :].unsqueeze(1).to_broadcast([128, 16, 32])
            sinib = c.sinI[:, tb, :].unsqueeze(1).to_broadcast([128, 16, 32])
            v4 = lambda t: t[:].rearrange("p (h two j) -> p h two j", two=2, j=64)
            v4i = lambda t: t[:].rearrange("p (h two j) -> p h two j", two=2, j=32)
            tqv = [t[:].rearrange("p (h j) -> p h j", j=64) for t in tq]
            tkv = [t[:].rearrange("p (h j) -> p h j", j=64) for t in tk]
            tqiv = [t[:].rearrange("p (h j) -> p h j", j=32) for t in tq]
            _rope(T, "dve", V, v4(qr), v4(bq[i]), cosb, sinb, tqv, b_bq[i], b_qr, b_tq, c.b_rope)
            _rope(T, "pool", G, v4(kr), v4(bk[i]), cosb, sinb, tkv, b_bk[i], b_kr, b_tk, c.b_rope)
            _rope(T, "dve", V, v4i(iqr), v4i(iq[i]), cosib, sinib, tqiv, b_iq[i], b_iqr, b_tq, c.b_rope)
            T.op("act", lambda: A.copy(vb[i][:], bv[i][:]), reads=[b_bv[i]], writes=[b_vb[i]])
            T.dma("pool", c.v_d[tsl, :], vb[i][:], reads=[b_vb[i]])
            for (src, bsrc, pi, dst, bdst, dd) in ((qr, b_qr, 0, qTt[i], b_qTt[i], c.qT_d), (kr, b_kr, 1, kTt[i], b_kTt[i], c.kT_d),
                                                   (iqr, b_iqr, 0, iqTt[i], b_iqTt[i], c.iqT_d)):
                T.op("pe", lambda src=src, pi=pi: _grp([(lambda h=h: PE.transpose(psb[pi][:, h * 128:(h + 1) * 128],
                                                                                  src[:, h * 128:(h + 1) * 128], c.identb[:]))
                                                        for h in range(8)]),
                     reads=[bsrc, c.b_cst], writes=[b_psb[pi]])
                T.op("act", lambda dst=dst, pi=pi: A.copy(dst[:].rearrange("p h t -> p (h t)"), psb[pi][:]),
                     reads=[b_psb[pi]], writes=[bdst])
                T.dma("pool", dd[:, :, tsl].rearrange("h p t -> p h t"), dst[:], reads=[bdst])
            ik = ikw[i][:, 0:64]
            T.op("dve", lambda: V.tensor_reduce(st1[:, 0:1], ik, AX.X, ALU.add), reads=[b_ikw[i]], writes=[b_st1])
            T.op("dve", lambda: V.tensor_scalar(st1[:, 1:2], st1[:, 0:1], -1.0 / 64.0, None, ALU.mult), reads=[b_st1], writes=[b_st1])
            T.op("dve", lambda: V.tensor_scalar(xc[:], ik, st1[:, 1:2], None, ALU.add), reads=[b_ikw[i], b_st1], writes=[b_xc])
            T.op("dve", lambda: V.tensor_tensor(sq[:], xc[:], xc[:], ALU.mult), reads=[b_xc], writes=[b_sq])
            T.op("dve", lambda: V.tensor_reduce(st1[:, 2:3], sq[:], AX.X, ALU.add), reads=[b_sq], writes=[b_st1])
            T.op("act", lambda: A.activation(st1[:, 3:4], st1[:, 2:3], AF.Sqrt, bias=LN_EPS, scale=1.0 / 64.0),
                 reads=[b_st1], writes=[b_st1])
            T.op("dve", lambda: V.reciprocal(st1[:, 4:5], st1[:, 3:4]), reads=[b_st1], writes=[b_st1])
            T.op("dve", lambda: V.tensor_scalar(xc[:], xc[:], st1[:, 4:5], None, ALU.mult), reads=[b_xc, b_st1], writes=[b_xc])
            T.op("dve", lambda: V.tensor_tensor(xc[:], xc[:], kn[:, 0, :], ALU.mult), reads=[b_xc, b_kn], writes=[b_xc])
            T.op("dve", lambda: V.tensor_tensor(xc[:], xc[:], kn[:, 1, :], ALU.add), reads=[b_xc, b_kn], writes=[b_xc])
            ci, si = c.cosI[:, tb, :], c.sinI[:, tb, :]
            x1, x2 = xc[:, 0:32], xc[:, 32:64]
            T.op("dve", lambda: V.tensor_tensor(t3[0][:], x1, ci, ALU.mult), reads=[b_xc, c.b_rope], writes=[b_t3[0]])
            T.op("dve", lambda: V.tensor_tensor(t3[1][:], x2, si, ALU.mult), reads=[b_xc, c.b_rope], writes=[b_t3[1]])
            T.op("dve", lambda: V.tensor_tensor(ikr[:, 0:32], t3[0][:], t3[1][:], ALU.subtract), reads=b_t3, writes=[b_ikr])
            T.op("dve", lambda: V.tensor_tensor(ikr[:, 64:96], t3[0][:], t3[1][:], ALU.subtract), reads=b_t3, writes=[b_ikr])
            T.op("dve", lambda: V.tensor_tensor(t3[0][:], x1, si, ALU.mult), reads=[b_xc, c.b_rope], writes=[b_t3[0]])
            T.op("dve", lambda: V.tensor_tensor(t3[1][:], x2, ci, ALU.mult), reads=[b_xc, c.b_rope], writes=[b_t3[1]])
            T.op("dve", lambda: V.tensor_tensor(ikr[:, 32:64], t3[0][:], t3[1][:], ALU.add), reads=b_t3, writes=[b_ikr])
            T.op("dve", lambda: V.tensor_tensor(ikr[:, 96:128], t3[0][:], t3[1][:], ALU.add), reads=b_t3, writes=[b_ikr])
            T.op("pe", lambda: PE.transpose(psb[1][:, 0:128], ikr[:], c.identb[:]), reads=[b_ikr, c.b_cst], writes=[b_psb[1]])
            T.op("act", lambda: A.copy(ikT[i][:], psb[1][:, 0:128]), reads=[b_psb[1]], writes=[b_ikT[i]])
            T.dma("pool", c.ikT_d[:, tsl], ikT[i][:], reads=[b_ikT[i]])
            T.op("dve", lambda: V.tensor_scalar(iwt[i][:], ikw[i][:, 64:80], IDX_SCALE, None, ALU.mult),
                 reads=[b_ikw[i]], writes=[b_iwt[i]])
            T.dma("pool", c.iw_d[tsl, :], iwt[i][:], reads=[b_iwt[i]])
            yield tb
        T.barrier()


def phase_dsa(c, l):
    nc, T = c.nc, c.T
    PE, V, A, G = nc.tensor, nc.vector, nc.scalar, nc.gpsimd
    ps, b_ps = c.ps, c.b_ps
    psT, b_psT = c.psb[1], c.b_ps[7]
    with ExitStack() as es:
        mk = _tiles(es, nc)
        ikT2 = mk("ikT2", [128, SEQ], BF16)
        iqTt = [mk("iqTt%d" % i, [128, 8, 128], BF16) for i in range(2)]
        qTt = [mk("qTt%d" % i, [128, 8, 128], BF16) for i in range(2)]
        iwt = [mk("iwt%d" % i, [128, 16], F32) for i in range(2)]
        diag = [mk("diag%d" % i, [128, 16, 128], BF16) for i in range(2)]
        sc = [mk("sc%d" % i, [128, SEQ], F32) for i in range(2)]
        work = mk("work", [128, SEQ], F32)
        rl = [mk("rl%d" % i, [128, 512], BF16) for i in range(2)]
        m8 = mk("m8", [128, 8], F32)
        maskt = mk("maskt", [128, SEQ], BF16)
        maskT = [mk("maskT%d" % i, [128, NBLK, 128], BF16) for i in range(2)]
        kTh = [mk("kTh%d" % i, [128, SEQ], BF16) for i in range(2)]
        vh = [mk("vh%d" % i, [128, NBLK, 128], BF16) for i in range(2)]
        Et = [mk("Et%d" % i, [128, 512], BF16) for i in range(2)]
        PM = [mk("PM%d" % i, [128, 512], BF16) for i in range(2)]
        rec = [mk("rec%d" % i, [128, 128], F32) for i in range(2)]
        nd = [mk("nd%d" % i, [128, 128], F32) for i in range(2)]
        b_nd = T.bufs(2)
        ybt = [mk("ybt%d" % i, [128, 8, 128], BF16) for i in range(2)]
        b_ik = T.buf()
        b_iqT, b_qT, b_iw, b_rl, b_diag, b_sc, b_maskT = (T.bufs(2), T.bufs(2), T.bufs(2), T.bufs(2), T.bufs(2),
                                                          T.bufs(2), T.bufs(2))
        b_work, b_m8, b_maskt = T.bufs(3)
        b_kTh, b_vh, b_Et, b_PM, b_rec, b_ybt = T.bufs(2), T.bufs(2), T.bufs(2), T.bufs(2), T.bufs(2), T.bufs(2)
        T.dma("sp", ikT2[:], c.ikT_d[:, :], writes=[b_ik])
        st = {"hcnt": 0, "gcnt": 0}

        def score(qb):
            i = qb % 2
            L = 128 * (qb + 1)
            tsl = slice(qb * 128, (qb + 1) * 128)
            T.dma("sp", iqTt[i][:], c.iqT_d[:, :, tsl].rearrange("h p t -> p h t"), writes=[b_iqT[i]])
            T.dma("sp", iwt[i][:], c.iw_d[tsl, :], writes=[b_iw[i]])
            for hi in range(IDX_HEADS):
                T.op("pool", lambda hi=hi: G.tensor_scalar(diag[i][:, hi, :], c.identb[:], iwt[i][:, hi:hi + 1], None, ALU.mult),
                     reads=[b_iw[i], c.b_cst], writes=[b_diag[i]])
            nst = (L + 511) // 512
            for s_t in range(nst):
                s0 = s_t * 512
                n = min(512, L - s0)

                def idx(hi):
                    pair, half, pj = hi // 2, hi % 2, hi % 2
                    T.op("pe", lambda: PE.matmul(ps[pj][:, 0:n], iqTt[i][half * 64:(half + 1) * 64, pair, :],
                                                 ikT2[half * 64:(half + 1) * 64, s0:s0 + n], start=True, stop=True),
                         reads=[b_iqT[i], b_ik], writes=[b_ps[pj]])
                idx(0)
                for hi in range(IDX_HEADS):
                    pj = hi % 2
                    if hi + 1 < IDX_HEADS:
                        idx(hi + 1)
                    T.op("act", lambda pj=pj: A.activation(rl[pj][:, 0:n], ps[pj][:, 0:n], AF.Relu),
                         reads=[b_ps[pj]], writes=[b_rl[pj]])
                    T.op("pe", lambda pj=pj, hi=hi: PE.matmul(ps[2][:, 0:n], diag[i][:, hi, :], rl[pj][:, 0:n],
                                                              start=(hi == 0), stop=(hi == IDX_HEADS - 1)),
                         reads=[b_diag[i], b_rl[pj]], writes=[b_ps[2]])
                T.op("act", lambda: A.copy(sc[i][:, s0:s0 + n], ps[2][:, 0:n]), reads=[b_ps[2]], writes=[b_sc[i]])
            T.op("pool", lambda: G.memset(sc[i][0:64, L - 64:L], NEG), reads=[b_sc[i]], writes=[b_sc[i]])

        def topk_rounds(qb):
            i = qb % 2
            L = 128 * (qb + 1)
            if qb < 2:
                return
            nr = TOPK // 8
            for r in range(nr):
                src = sc[i] if r == 0 else work
                bsrc = b_sc[i] if r == 0 else b_work
                T.op("dve", lambda src=src: V.max(out=m8[:], in_=src[:, 0:L]), reads=[bsrc], writes=[b_m8])
                if r < nr - 1:
                    T.op("dve", lambda src=src: V.match_replace(out=work[:, 0:L], in_to_replace=m8[:], in_values=src[:, 0:L],
                                                                imm_value=-3.0e38),
                         reads=[bsrc, b_m8], writes=[b_work])
                yield r

        def mask(qb):
            i = qb % 2
            L = 128 * (qb + 1)
            nkb = qb + 1
            if qb >= 2:
                T.op("dve", lambda: V.tensor_scalar(maskt[:, 0:L], sc[i][:, 0:L], m8[:, 7:8], None, ALU.is_ge),
                     reads=[b_sc[i], b_m8], writes=[b_maskt])
            else:
                T.op("dve", lambda: V.tensor_scalar(maskt[:, 0:L], sc[i][:, 0:L], -1.0e29, None, ALU.is_ge),
                     reads=[b_sc[i]], writes=[b_maskt])
            for g0 in range(0, nkb, 8):
                nb = min(8, nkb - g0)
                T.op("pe", lambda g0=g0, nb=nb: _grp([(lambda j=j: PE.transpose(
                    psT[:, j * 128:(j + 1) * 128], maskt[:, (g0 + j) * 128:(g0 + j + 1) * 128], c.identb[:])) for j in range(nb)]),
                     reads=[b_maskt, c.b_cst], writes=[b_psT])
                T.op("act", lambda g0=g0, nb=nb: A.copy(maskT[i][:, g0:g0 + nb, :].rearrange("p b t -> p (b t)"),
                                                        psT[:, 0:nb * 128]),
                     reads=[b_psT], writes=[b_maskT[i]])

        def attention(qb):
            i = qb % 2
            L = 128 * (qb + 1)
            nkb = qb + 1
            tsl = slice(qb * 128, (qb + 1) * 128)
            deferred = []
            T.dma("sp", qTt[i][:], c.qT_d[:, :, tsl].rearrange("h p t -> p h t"), writes=[b_qT[i]])
            for h in range(8):
                hi2 = st["hcnt"] % 2
                bnk = 4 + (st["hcnt"] % 2)
                ri = st["hcnt"] % 2
                st["hcnt"] += 1
                T.dma("sp", kTh[hi2][:, 0:L], c.kT_d[h, :, 0:L], writes=[b_kTh[hi2]])
                T.dma("sp", vh[hi2][:, 0:nkb, :], c.v_d[0:L, h * 128:(h + 1) * 128].rearrange("(b p) d -> p b d", p=128),
                      writes=[b_vh[hi2]])
                num = ps[bnk][:, 0:128]
                den = ps[bnk][:, 128:256]
                groups = [(g0, min(4, nkb - g0)) for g0 in range(0, nkb, 4)]
                pend = None

                def pv(g0, nb, gi):
                    fns = []
                    for j in range(nb):
                        kb = g0 + j
                        fns.append(lambda j=j, kb=kb: PE.matmul(num, vh[hi2][:, kb, :], PM[gi][:, j * 128:(j + 1) * 128],
                                                                start=(kb == 0), stop=(kb == nkb - 1), skip_group_check=True))
                        fns.append(lambda j=j, kb=kb: PE.matmul(den, c.onesb[:], PM[gi][:, j * 128:(j + 1) * 128],
                                                                start=False, stop=(kb == nkb - 1), skip_group_check=True))
                    T.op("pe", lambda: _grp(fns), reads=[b_vh[hi2], b_PM[gi], c.b_cst], writes=[b_ps[bnk]])

                for (g0, nb) in groups:
                    gi = st["gcnt"] % 2
                    st["gcnt"] += 1
                    T.op("pe", lambda g0=g0, nb=nb, gi=gi: _grp([(lambda j=j: PE.matmul(
                        ps[3 + 3 * gi][:, j * 128:(j + 1) * 128], kTh[hi2][:, (g0 + j) * 128:(g0 + j + 1) * 128], qTt[i][:, h, :],
                        start=True, stop=True)) for j in range(nb)]),
                         reads=[b_kTh[hi2], b_qT[i]], writes=[b_ps[3 + 3 * gi]])
                    T.op("act", lambda nb=nb, gi=gi: A.activation(Et[gi][:, 0:nb * 128], ps[3 + 3 * gi][:, 0:nb * 128], AF.Exp,
                                                                  scale=ATTN_SCALE),
                         reads=[b_ps[3 + 3 * gi]], writes=[b_Et[gi]])
                    T.op("pool", lambda g0=g0, nb=nb, gi=gi: G.tensor_tensor(
                        PM[gi][:, 0:nb * 128], Et[gi][:, 0:nb * 128],
                        maskT[i][:, g0:g0 + nb, :].rearrange("p b t -> p (b t)"), ALU.mult),
                         reads=[b_Et[gi], b_maskT[i]], writes=[b_PM[gi]])
                    if pend is not None:
                        pv(*pend)
                    pend = (g0, nb, gi)
                pv(*pend)

                T.op("act", lambda: A.copy(nd[ri][:], num), reads=[b_ps[bnk]], writes=[b_nd[ri]])
                T.op("act", lambda: A.activation(rec[ri][:], den, AF.Ln), reads=[b_ps[bnk]], writes=[b_rec[ri]])
                T.op("act", lambda: A.activation(rec[ri][:], rec[ri][:], AF.Exp, scale=-1.0), reads=[b_rec[ri]], writes=[b_rec[ri]])
                T.op("pool", lambda: G.tensor_tensor(ybt[i][:, h, :], nd[ri][:], rec[ri][:], ALU.mult),
                     reads=[b_nd[ri], b_rec[ri]], writes=[b_ybt[i]])
            T.dma("pool", c.ybT_d[:, :, tsl].rearrange("h p t -> p h t"), ybt[i][:], reads=[b_ybt[i]])
            return deferred

        bg = []
        per_it = (len(bg) + NBLK - 1) // NBLK
        score(0)
        for it in range(0, NBLK + 1):
            qs, qt, qa = it + 1, it, it - 1
            for _ in range(per_it):
                if bg:
                    bg.pop(0)()
            tk = topk_rounds(qt) if qt < NBLK else iter(())
            for _ in range(3):
                next(tk, None)
            deferred = attention(qa) if qa >= 0 else []
            if qs < NBLK:
                score(qs)
            cnt = 0
            for _ in tk:
                cnt += 1
                if cnt % 3 == 0 and deferred:
                    deferred.pop(0)()
            while deferred:
                deferred.pop(0)()
            if qt < NBLK:
                mask(qt)
        T.barrier()


_ln_pending = [None]


def _ln_stats_pe(c, kc, zb, b_zb):
    nc, T = c.nc, c.T
    PE = nc.tensor
    ps, b_ps = c.ps, c.b_ps
    zi = kc % 2
    T.op("pe", lambda: PE.matmul(ps[4][:], c.onesb[:], zb[zi][:, 0, :], start=(kc == 0), stop=(kc == KC - 1)),
         reads=[b_zb[zi], c.b_cst], writes=[b_ps[4]])
    T.op("pe", lambda: PE.matmul(ps[5][:], c.onesb[:], zb[zi][:, 1, :], start=(kc == 0), stop=(kc == KC - 1)),
         reads=[b_zb[zi], c.b_cst], writes=[b_ps[5]])


def _ln_stats(c, z, b_z, kc, zb, b_zb):
    nc, T = c.nc, c.T
    A = nc.scalar
    zi = kc % 2
    if _ln_pending[0] is not None:
        _ln_stats_pe(c, _ln_pending[0], zb, b_zb)
    T.op("act", lambda: A.copy(zb[zi][:, 0, :], z[:, kc, :]), reads=[b_z[kc]], writes=[b_zb[zi]])
    T.op("act", lambda: A.activation(zb[zi][:, 1, :], z[:, kc, :], AF.Square), reads=[b_z[kc]], writes=[b_zb[zi]])
    _ln_pending[0] = kc
    if kc == KC - 1:
        _ln_stats_pe(c, kc, zb, b_zb)
        _ln_pending[0] = None


def _ln_finish(c, z, b_z, l, jg, xb, b_xb, scratch, b_scr):
    nc, T = c.nc, c.T
    V, A = nc.vector, nc.scalar
    ps, b_ps = c.ps, c.b_ps
    mean, rstd, msq = scratch
    T.op("dve", lambda: V.tensor_scalar(mean[:], ps[4][:], 1.0 / D_MODEL, None, ALU.mult), reads=[b_ps[4]], writes=[b_scr[0]])
    T.op("dve", lambda: V.tensor_tensor(msq[:], mean[:], mean[:], ALU.mult), reads=[b_scr[0]], writes=[b_scr[2]])
    T.op("dve", lambda: V.scalar_tensor_tensor(rstd[:], ps[5][:], 1.0 / D_MODEL, msq[:], ALU.mult, ALU.subtract),
         reads=[b_ps[5], b_scr[2]], writes=[b_scr[1]])
    T.op("act", lambda: A.activation(rstd[:], rstd[:], AF.Sqrt, bias=LN_EPS, scale=1.0), reads=[b_scr[1]], writes=[b_scr[1]])
    T.op("dve", lambda: V.reciprocal(rstd[:], rstd[:]), reads=[b_scr[1]], writes=[b_scr[1]])
    for kc in range(KC):
        T.op("dve", lambda kc=kc: V.tensor_tensor(z[:, kc, :], z[:, kc, :], mean[:], ALU.subtract),
             reads=[b_z[kc], b_scr[0]], writes=[b_z[kc]])
        T.op("dve", lambda kc=kc: V.tensor_tensor(z[:, kc, :], z[:, kc, :], rstd[:], ALU.mult),
             reads=[b_z[kc], b_scr[1]], writes=[b_z[kc]])
        T.op("act", lambda kc=kc: A.activation(z[:, kc, :], z[:, kc, :], AF.Identity, bias=c.lnp[:, l, jg + 1, kc:kc + 1],
                                               scale=c.lnp[:, l, jg, kc:kc + 1]),
             reads=[b_z[kc], c.b_cst], writes=[b_z[kc]])
        T.op("act", lambda kc=kc: A.copy(xb[:, kc, :], z[:, kc, :]), reads=[b_z[kc]], writes=[b_xb[kc]])


def _wload(c, dst, bdst, src2d, ncols):
    c.T.dma("sp", dst[:, :, 0:ncols], src2d.rearrange("(k p) n -> p k n", p=128), writes=[bdst])


def phase_stageA(c, l):
    nc, T = c.nc, c.T
    PE, V, A, G = nc.tensor, nc.vector, nc.scalar, nc.gpsimd
    ps, b_ps = c.ps, c.b_ps
    WC = 256
    with ExitStack() as es:
        mk = _tiles(es, nc)
        xT = mk("xT", [128, KC, TT], BF16)
        oN = mk("oN", [128, 8, TT], BF16)
        yb = mk("yb", [128, 8, TT], BF16)
        ya = mk("ya", [128, 8, TT], BF16)
        mg = mk("mg", [128, KC, TT], BF16)
        z = mk("z", [128, KC, TT], F32)
        xb = mk("xb", [128, KC, TT], BF16)
        xr = [mk("xr%d" % i, [128, TT], F32) for i in range(2)]
        w16 = [mk("w16_%d" % i, [128, KC, WC], BF16) for i in range(4)]
        w8 = [mk("w8_%d" % i, [128, 8, WC], BF16) for i in range(4)]
        sg = [mk("sg%d" % i, [128, TT], F32) for i in range(2)]
        tm = [mk("tm%d" % i, [128, TT], F32) for i in range(2)]
        scratch = [mk("lns%d" % i, [128, TT], F32) for i in range(3)]
        zb = [mk("zb%d" % i, [128, 2, TT], BF16) for i in range(2)]
        b_xT, b_oN, b_yb = T.buf(), T.buf(), T.buf()
        b_ya = T.bufs(8)
        b_mg = T.bufs(KC)
        b_z = T.bufs(KC)
        b_xb = T.bufs(KC)
        b_xr, b_w16, b_w8, b_sg, b_tm, b_scr, b_zb = T.bufs(2), T.bufs(4), T.bufs(4), T.bufs(2), T.bufs(2), T.bufs(3), T.bufs(2)
        cnt = 0
        for tt in range(SEQ // TT):
            tsl = slice(tt * TT, (tt + 1) * TT)
            T.dma("sp", xT[:], c.xT_d[:, :, tsl].rearrange("k p t -> p k t"), writes=[b_xT])
            T.dma("sp", oN[:], c.oN_d[:, :, tsl].rearrange("k p t -> p k t"), writes=[b_oN])
            T.dma("sp", yb[:], c.ybT_d[:, :, tsl].rearrange("k p t -> p k t"), writes=[b_yb])
            for g in range(A_W // WC):
                wi = g % 2
                _wload(c, w16[wi], b_w16[wi], c.wb_in[l, :, 3072 + g * WC:3072 + (g + 1) * WC], WC)
                for j in range(WC // 128):
                    h = g * (WC // 128) + j
                    pi = cnt % 4
                    cnt += 1
                    T.op("pe", lambda wi=wi, j=j, pi=pi: _grp([(lambda kc=kc: PE.matmul(
                        ps[pi][:], w16[wi][:, kc, j * 128:(j + 1) * 128], xT[:, kc, :], start=(kc == 0), stop=(kc == KC - 1)))
                        for kc in range(KC)]), reads=[b_w16[wi], b_xT], writes=[b_ps[pi]])
                    si = cnt % 2
                    T.op("act", lambda pi=pi, si=si: A.activation(sg[si][:], ps[pi][:], AF.Silu), reads=[b_ps[pi]], writes=[b_sg[si]])
                    T.op("dve", lambda h=h, si=si: V.tensor_tensor(ya[:, h, :], sg[si][:], oN[:, h, :], ALU.mult),
                         reads=[b_sg[si], b_oN], writes=[b_ya[h]])
            for g in range(D_MODEL // WC):
                wi = g % 2
                c0 = g * WC
                _wload(c, w16[wi], b_w16[wi], c.wb_in[l, :, 8272 + c0:8272 + c0 + WC], WC)
                _wload(c, w16[2 + wi], b_w16[2 + wi], c.wb_in[l, :, 10320 + c0:10320 + c0 + WC], WC)
                _wload(c, w8[wi], b_w8[wi], c.wb_ba[l, :, c0:c0 + WC], WC)
                _wload(c, w8[2 + wi], b_w8[2 + wi], c.wb_bb[l, :, c0:c0 + WC], WC)
                for j in range(WC // 128):
                    dc = g * (WC // 128) + j
                    js = slice(j * 128, (j + 1) * 128)
                    T.op("pe", lambda wi=wi, js=js: _grp([(lambda kc=kc: PE.matmul(
                        ps[0][:], w16[wi][:, kc, js], xT[:, kc, :], start=(kc == 0), stop=(kc == KC - 1))) for kc in range(KC)]),
                         reads=[b_w16[wi], b_xT], writes=[b_ps[0]])
                    T.op("pe", lambda wi=wi, js=js: _grp([(lambda kc=kc: PE.matmul(
                        ps[1][:], w8[wi][:, kc, js], ya[:, kc, :], start=(kc == 0), stop=(kc == 7))) for kc in range(8)]),
                         reads=[b_w8[wi]] + b_ya, writes=[b_ps[1]])
                    T.op("pe", lambda wi=wi, js=js: _grp([(lambda kc=kc: PE.matmul(
                        ps[2][:], w16[2 + wi][:, kc, js], xT[:, kc, :], start=(kc == 0), stop=(kc == KC - 1))) for kc in range(KC)]),
                         reads=[b_w16[2 + wi], b_xT], writes=[b_ps[2]])
                    T.op("pe", lambda wi=wi, js=js: _grp([(lambda kc=kc: PE.matmul(
                        ps[3][:], w8[2 + wi][:, kc, js], yb[:, kc, :], start=(kc == 0), stop=(kc == 7))) for kc in range(8)]),
                         reads=[b_w8[2 + wi], b_yb], writes=[b_ps[3]])
                    T.op("act", lambda: A.activation(sg[0][:], ps[0][:], AF.Sigmoid), reads=[b_ps[0]], writes=[b_sg[0]])
                    T.op("act", lambda: A.activation(sg[1][:], ps[2][:], AF.Sigmoid), reads=[b_ps[2]], writes=[b_sg[1]])
                    T.op("dve", lambda: V.tensor_tensor(tm[0][:], sg[0][:], ps[1][:], ALU.mult), reads=[b_sg[0], b_ps[1]], writes=[b_tm[0]])
                    T.op("dve", lambda: V.tensor_tensor(tm[1][:], sg[1][:], ps[3][:], ALU.mult), reads=[b_sg[1], b_ps[3]], writes=[b_tm[1]])
                    T.op("pool", lambda dc=dc: G.tensor_tensor(mg[:, dc, :], tm[0][:], tm[1][:], ALU.add), reads=b_tm, writes=[b_mg[dc]])
            for g in range(D_MODEL // WC):
                wi = g % 2
                c0 = g * WC
                _wload(c, w16[wi], b_w16[wi], c.wb_out[l, :, c0:c0 + WC], WC)
                for j in range(WC // 128):
                    dc = g * (WC // 128) + j
                    js = slice(j * 128, (j + 1) * 128)
                    pi = dc % 4
                    ri = dc % 2
                    T.dma("sp", xr[ri][:], c.xres_d[dc, :, tsl], writes=[b_xr[ri]])
                    T.op("pe", lambda wi=wi, js=js, pi=pi: _grp([(lambda kc=kc: PE.matmul(
                        ps[pi][:], w16[wi][:, kc, js], mg[:, kc, :], start=(kc == 0), stop=(kc == KC - 1))) for kc in range(KC)]),
                         reads=[b_w16[wi]] + b_mg, writes=[b_ps[pi]])
                    T.op("dve", lambda dc=dc, ri=ri, pi=pi: V.scalar_tensor_tensor(z[:, dc, :], xr[ri][:], ALPHA, ps[pi][:],
                                                                                   ALU.mult, ALU.add),
                         reads=[b_xr[ri], b_ps[pi]], writes=[b_z[dc]])
                    _ln_stats(c, z, b_z, dc, zb, b_zb)
            _ln_finish(c, z, b_z, l, 0, xb, b_xb, scratch, b_scr)
            T.dma("pool", c.x1res_d[:, :, tsl].rearrange("k p t -> p k t"), z[:], reads=b_z)
            T.dma("pool", c.x1T_d[:, :, tsl].rearrange("k p t -> p k t"), xb[:], reads=b_xb)
        T.barrier()


def phase_stageB(c, l, last):
    nc, T = c.nc, c.T
    PE, V, A, G = nc.tensor, nc.vector, nc.scalar, nc.gpsimd
    ps, b_ps = c.ps, c.b_ps
    WC = 256
    NH = D_FF // 128
    with ExitStack() as es:
        mk = _tiles(es, nc)
        xb = mk("xb", [128, KC, TT], BF16)
        z = mk("z", [128, KC, TT], F32)
        hT = mk("hT", [128, NH, TT], BF16)
        pT = mk("pT", [128, 2, TT], BF16)
        wbuf = [mk("wbuf%d" % i, [128, NH, WC], BF16) for i in range(2)]
        wbuf2 = [mk("wbuf2_%d" % i, [128, KC, WC], BF16) for i in range(2)]
        sg = [mk("sg%d" % i, [128, TT], F32) for i in range(2)]
        tm = [mk("tm%d" % i, [128, TT], F32) for i in range(2)]
        scratch = [mk("lns%d" % i, [128, TT], F32) for i in range(3)]
        zb = [mk("zb%d" % i, [128, 2, TT], BF16) for i in range(2)]
        b_xbw = T.buf()
        b_xb = T.bufs(KC)
        b_z = T.bufs(KC)
        b_h = T.bufs(NH)
        b_pT = T.buf()
        b_w, b_w2, b_sg, b_tm, b_scr, b_zb = T.bufs(2), T.bufs(2), T.bufs(2), T.bufs(2), T.bufs(3), T.bufs(2)
        for tt in range(SEQ // TT):
            tsl = slice(tt * TT, (tt + 1) * TT)
            T.dma("sp", xb[:], c.x1T_d[:, :, tsl].rearrange("k p t -> p k t"), writes=b_xb)
            T.dma("sp", z[:], c.x1res_d[:, :, tsl].rearrange("k p t -> p k t"), writes=b_z)
            T.dma("pool", pT[:], c.pT_in[l, :, :, tsl].rearrange("k p t -> p k t"), writes=[b_pT])
            for g in range(D_FF // WC):
                wi = g % 2
                c0 = g * WC
                _wload(c, wbuf[wi][:, 0:KC, :], b_w[wi], c.wb_fg[l, :, c0:c0 + WC], WC)
                _wload(c, wbuf2[wi], b_w2[wi], c.wb_fu[l, :, c0:c0 + WC], WC)
                for j in range(WC // 128):
                    hc = g * (WC // 128) + j
                    js = slice(j * 128, (j + 1) * 128)
                    p0, p1 = (0, 1) if hc % 2 == 0 else (2, 3)
                    si = hc % 2
                    T.op("pe", lambda wi=wi, js=js, p0=p0: _grp([(lambda kc=kc: PE.matmul(
                        ps[p0][:], wbuf[wi][:, kc, js], xb[:, kc, :], start=(kc == 0), stop=(kc == KC - 1))) for kc in range(KC)]),
                         reads=[b_w[wi]] + b_xb, writes=[b_ps[p0]])
                    T.op("pe", lambda wi=wi, js=js, p1=p1: _grp([(lambda kc=kc: PE.matmul(
                        ps[p1][:], wbuf2[wi][:, kc, js], xb[:, kc, :], start=(kc == 0), stop=(kc == KC - 1))) for kc in range(KC)]),
                         reads=[b_w2[wi]] + b_xb, writes=[b_ps[p1]])
                    T.op("act", lambda si=si, p0=p0: A.activation(sg[si][:], ps[p0][:], AF.Silu), reads=[b_ps[p0]], writes=[b_sg[si]])
                    T.op("dve", lambda hc=hc, si=si, p1=p1: V.tensor_tensor(hT[:, hc, :], sg[si][:], ps[p1][:], ALU.mult),
                         reads=[b_sg[si], b_ps[p1]], writes=[b_h[hc]])
            for g in range(D_MODEL // WC):
                wi = g % 2
                c0 = g * WC
                _wload(c, wbuf[wi], b_w[wi], c.wb_fd[l, :, c0:c0 + WC], WC)
                for j in range(WC // 128):
                    dc = g * (WC // 128) + j
                    js = slice(j * 128, (j + 1) * 128)
                    pi = dc % 4
                    T.op("pe", lambda wi=wi, js=js, pi=pi: _grp([(lambda kc=kc: PE.matmul(
                        ps[pi][:], wbuf[wi][:, kc, js], hT[:, kc, :], start=(kc == 0), stop=(kc == NH - 1))) for kc in range(NH)]),
                         reads=[b_w[wi]] + b_h, writes=[b_ps[pi]])
                    T.op("dve", lambda dc=dc, pi=pi: V.scalar_tensor_tensor(z[:, dc, :], z[:, dc, :], ALPHA, ps[pi][:], ALU.mult, ALU.add),
                         reads=[b_z[dc], b_ps[pi]], writes=[b_z[dc]])
                    _ln_stats(c, z, b_z, dc, zb, b_zb)
            _ln_finish(c, z, b_z, l, 2, xb, b_xb, scratch, b_scr)
            for g in range(D_MODEL // WC):
                wi = g % 2
                c0 = g * WC
                _wload(c, wbuf2[wi], b_w2[wi], c.wb_pg[l, :, c0:c0 + WC], WC)
                _wload(c, wbuf[wi][:, 0:2, :], b_w[wi], c.wb_pp[l, :, c0:c0 + WC], WC)
                for j in range(WC // 128):
                    dc = g * (WC // 128) + j
                    js = slice(j * 128, (j + 1) * 128)
                    p0, p1 = (0, 1) if dc % 2 == 0 else (2, 3)
                    si = dc % 2
                    T.op("pe", lambda wi=wi, js=js, p0=p0: _grp([(lambda kc=kc: PE.matmul(
                        ps[p0][:], wbuf2[wi][:, kc, js], xb[:, kc, :], start=(kc == 0), stop=(kc == KC - 1))) for kc in range(KC)]),
                         reads=[b_w2[wi]] + b_xb, writes=[b_ps[p0]])
                    T.op("pe", lambda wi=wi, js=js, p1=p1: _grp([(lambda kc=kc: PE.matmul(
                        ps[p1][:], wbuf[wi][:, kc, js], pT[:, kc, :], start=(kc == 0), stop=(kc == 1))) for kc in range(2)]),
                         reads=[b_w[wi], b_pT], writes=[b_ps[p1]])
                    T.op("act", lambda si=si, p0=p0: A.activation(sg[si][:], ps[p0][:], AF.Sigmoid), reads=[b_ps[p0]], writes=[b_sg[si]])
                    T.op("dve", lambda si=si, p1=p1: V.tensor_tensor(tm[si][:], sg[si][:], ps[p1][:], ALU.mult),
                         reads=[b_sg[si], b_ps[p1]], writes=[b_tm[si]])
                    T.op("dve", lambda dc=dc, si=si: V.scalar_tensor_tensor(z[:, dc, :], z[:, dc, :], ALPHA, tm[si][:], ALU.mult, ALU.add),
                         reads=[b_z[dc], b_tm[si]], writes=[b_z[dc]])
                    _ln_stats(c, z, b_z, dc, zb, b_zb)
            _ln_finish(c, z, b_z, l, 4, xb, b_xb, scratch, b_scr)
            if last:
                T.dma("pool", c.out_d[:, :, tsl].rearrange("k p t -> p k t"), z[:], reads=b_z)
            else:
                T.dma("pool", c.xres_d[:, :, tsl].rearrange("k p t -> p k t"), z[:], reads=b_z)
                T.dma("pool", c.xT_d[:, :, tsl].rearrange("k p t -> p k t"), xb[:], reads=b_xb)
        T.barrier()


def build_full(n_layers=DEPTH, debug=False):
    c = build_program(n_layers, debug)
    c.n_layers = n_layers
    phase_init(c)
    for l in range(n_layers):
        phase_proj(c, l)
        with ExitStack() as shared_es:
            c.shared_es = shared_es
            gens = [phase_hgrn(c, l), phase_dsa_prep(c, l)]
            while gens:
                for g in list(gens):
                    try:
                        next(g)
                    except StopIteration:
                        gens.remove(g)
            c.T.barrier()
        phase_dsa(c, l)
        phase_stageA(c, l)
        phase_stageB(c, l, last=(l == n_layers - 1))
    c.T.barrier()
    return c


def _consts():
    cs = np.zeros((128, 6, 128), np.float32)
    cs[:, 0, :] = np.eye(128, dtype=np.float32)
    s = np.arange(128)[:, None]
    t = np.arange(128)[None, :]
    cs[:, 1, :] = ((s // 64 == t // 64) & (s <= t)).astype(np.float32)
    cs[:, 2, :] = 1.0
    cs[:64, 3, 0] = 1.0
    cs[64:, 3, 1] = 1.0
    invf = np.zeros((128, 96), np.float32)
    invf[:, 0:64] = (10000.0 ** (-np.arange(64, dtype=np.float32) / np.float32(64))).astype(np.float32)[None, :]
    invf[:, 64:96] = (10000.0 ** (-np.arange(32, dtype=np.float32) / np.float32(32))).astype(np.float32)[None, :]
    return cs, invf


def make_in_maps(inputs, n_cores=8):
    f = lambda a: np.ascontiguousarray(np.asarray(a, dtype=np.float32))
    cs, invf = _consts()
    shared = {k: f(inputs[k]) for k in ("w_in", "w_branch_a", "w_branch_b", "w_out", "w_ffn_gate", "w_ffn_up",
                                          "w_ffn_down", "w_ple_gate", "w_ple_proj")}
    shared["consts"] = cs
    shared["invf"] = invf
    shared["hlb_rep"] = np.ascontiguousarray(np.broadcast_to(f(inputs["hgrn_lower_bounds"])[None], (128, DEPTH, A_W)))
    kn = np.stack([f(inputs["idx_k_norm_g"]), f(inputs["idx_k_norm_b"])], axis=1)
    shared["kn_rep"] = np.ascontiguousarray(np.broadcast_to(kn[None], (128, DEPTH, 2, IDX_DIM)))
    shared["hng_t"] = np.ascontiguousarray(f(inputs["hgrn_norm_g"]).T)
    lnp = np.stack([f(inputs[k]) for k in ("ln_mix_g", "ln_mix_b", "ln_ffn_g", "ln_ffn_b", "ln_ple_g", "ln_ple_b")], axis=1)
    shared["lnp_t"] = np.ascontiguousarray(lnp.reshape(DEPTH, 6, KC, 128).transpose(3, 0, 1, 2))
    x = f(inputs["x"])
    p = f(inputs["p"])
    pos = np.asarray(inputs["positions"]).astype(np.int32)
    maps = []
    for core in range(n_cores):
        b = core // 2
        m = dict(shared)
        m["xT_in"] = np.ascontiguousarray(x[b].T.reshape(KC, 128, SEQ))
        m["pT_in"] = np.ascontiguousarray(p[:, b].transpose(0, 2, 1).reshape(DEPTH, 2, 128, SEQ))
        m["pos_in"] = np.ascontiguousarray(pos[b].reshape(NBLK, 128).T)
        maps.append(m)
    return maps


def kernel(**inputs):
    c = build_full(DEPTH)
    maps = make_in_maps(inputs)
    res = run_bass_kernel_spmd(c.nc, maps, core_ids=list(range(8)))
    out = np.empty((4, SEQ, D_MODEL), np.float32)
    for b in range(4):
        o0 = res.results[2 * b]["outT"].reshape(D_MODEL, SEQ)
        o1 = res.results[2 * b + 1]["outT"].reshape(D_MODEL, SEQ)
        out[b, : SEQ // 2] = o0[:, : SEQ // 2].T
        out[b, SEQ // 2:] = o1[:, SEQ // 2:].T
    return out
```
